# Optimizing a Trainium2 kernel written in Bass

```python
import jax, jax.numpy as jnp
from jax import lax
import numpy as np

D_MODEL = 1024
BATCH = 8
SEQ = 2048
DEPTH = 2
DEC_BATCH = 128
DEC_SEQ = 4
PAST_LEN = 16384
PAGE_SIZE = 128

N_EVEN = (DEPTH + 1) // 2
N_ODD = DEPTH // 2
NORM_EPS = 1e-6

POOL_WINDOWS = (2, 4, 8, 16)
POOL_GROUPS = len(POOL_WINDOWS)
POOL_GROUP_DIM = D_MODEL // 16
POOL_DIM = POOL_GROUPS * POOL_GROUP_DIM
POOL_BUF = max(POOL_WINDOWS) - 1

RWKV_HEAD_DIM = 64
RWKV_DIM = D_MODEL - POOL_DIM
RWKV_HEADS = RWKV_DIM // RWKV_HEAD_DIM
DECAY_LORA = 64
AAA_LORA = 64
GATE_LORA = 128
RWKV_PROJ = 3 * RWKV_DIM + DECAY_LORA + AAA_LORA + GATE_LORA
RWKV_GN_EPS = 64e-5
EVEN_PROJ = POOL_DIM + RWKV_PROJ
EVEN_MIX = POOL_DIM + RWKV_DIM

CHUNK = 128
GMLP_DIM = D_MODEL // 2
GMLP_HEADS = 4
GMLP_HEAD_DIM = GMLP_DIM // GMLP_HEADS
LN_EPS = 1e-5

LRU_DIM = D_MODEL // 2
LRU_BLOCKS = 8
LRU_BLOCK_DIM = LRU_DIM // LRU_BLOCKS
CONV_WIDTH = 4
LRU_C = 8.0
ODD_PROJ = 2 * GMLP_DIM + 2 * LRU_DIM
ODD_MIX = GMLP_DIM + LRU_DIM

D_FF = 4 * D_MODEL

kernel_name = 'pool_rwkv7_gmlp_rglru_hybrid_step'


def rmsnorm(x, g):
    xf = x.astype(jnp.float32)
    y = xf * lax.rsqrt(jnp.mean(xf * xf, -1, keepdims=True) + NORM_EPS) * g.astype(jnp.float32)
    return y.astype(x.dtype)


def pool_mixer(u, buf, start, w_grp, scale):
    B, T, _ = u.shape
    full = jnp.concatenate([buf.astype(jnp.float32), u.astype(jnp.float32)], 1)
    cs = jnp.concatenate([jnp.zeros((B, 1, POOL_DIM), jnp.float32), jnp.cumsum(full, 1)], 1)
    end = POOL_BUF + 1
    pos = start + jnp.arange(T)
    means = []
    for gi, w in enumerate(POOL_WINDOWS):
        ch = slice(gi * POOL_GROUP_DIM, (gi + 1) * POOL_GROUP_DIM)
        s = cs[:, end:end + T, ch] - cs[:, end - w:end - w + T, ch]
        cnt = jnp.minimum(w, pos + 1).astype(jnp.float32)
        means.append(s / cnt[None, :, None])
    d = (jnp.concatenate(means, -1) - full[:, POOL_BUF:]).reshape(B, T, POOL_GROUPS, POOL_GROUP_DIM)
    y = jnp.einsum('btgc,gcd->btgd', d, w_grp.astype(jnp.float32)).reshape(B, T, POOL_DIM)
    return y * scale.astype(jnp.float32), full[:, -POOL_BUF:]


def rwkv7_mixer(p, shift_prev, wkv0, mu, w0, w_w2, a0, a_w2, g_w2, k_k, k_a, r_k, gn_g, gn_b):
    f32 = jnp.float32
    B, T, _ = p.shape
    p = p.astype(f32)
    p_prev = jnp.concatenate([shift_prev.astype(f32)[:, None], p[:, :-1]], 1)
    xs = p + (p_prev - p) * mu
    splits = [RWKV_DIM, 2 * RWKV_DIM, 3 * RWKV_DIM, 3 * RWKV_DIM + DECAY_LORA, 3 * RWKV_DIM + DECAY_LORA + AAA_LORA]
    r, k, v, cw, ca, cg = jnp.split(xs, splits, axis=-1)
    w = -jax.nn.softplus(-(w0 + jnp.tanh(cw) @ w_w2)) - 0.5
    decay = jnp.exp(-jnp.exp(w))
    a = jax.nn.sigmoid(a0 + ca @ a_w2)
    g = jax.nn.sigmoid(cg) @ g_w2
    hs = lambda t: t.reshape(B, T, RWKV_HEADS, RWKV_HEAD_DIM)
    kk = hs(k * k_k)
    kk = kk / jnp.maximum(jnp.sqrt(jnp.sum(kk * kk, -1, keepdims=True)), 1e-12)
    k = k * (1.0 + (a - 1.0) * k_a)
    r, k, v, decay, a = hs(r), hs(k), hs(v), hs(decay), hs(a)

    def step(S, inp):
        r_t, k_t, v_t, w_t, kk_t, a_t = inp
        Sk = jnp.einsum('bhvk,bhk->bhv', S, kk_t)
        S = (S * w_t[:, :, None, :] - Sk[..., None] * (kk_t * a_t)[:, :, None, :]
             + v_t[..., None] * k_t[:, :, None, :])
        return S, jnp.einsum('bhvk,bhk->bhv', S, r_t)

    tm = lambda t: jnp.moveaxis(t, 1, 0)
    S_T, o = lax.scan(step, wkv0.astype(f32), (tm(r), tm(k), tm(v), tm(decay), tm(kk), tm(a)))
    o = jnp.moveaxis(o, 0, 1)
    m = jnp.mean(o, -1, keepdims=True)
    var = jnp.mean(jnp.square(o - m), -1, keepdims=True)
    o = ((o - m) * lax.rsqrt(var + RWKV_GN_EPS)).reshape(B, T, RWKV_DIM) * gn_g + gn_b
    bonus = jnp.sum(r * k * r_k, -1, keepdims=True) * v
    o = o + bonus.reshape(B, T, RWKV_DIM)
    return o * g, p[:, -1], S_T


def gmlp_mixer(q, ln_g, ln_b, ws, bs):
    f32 = jnp.float32
    B, T, _ = q.shape
    z = jax.nn.gelu(q.astype(f32))
    u, v = z[..., :GMLP_DIM], z[..., GMLP_DIM:]
    m = jnp.mean(v, -1, keepdims=True)
    var = jnp.mean(jnp.square(v - m), -1, keepdims=True)
    v = (v - m) * lax.rsqrt(var + LN_EPS) * ln_g + ln_b
    pad = (-T) % CHUNK
    nc = (T + pad) // CHUNK
    vh = jnp.pad(v, ((0, 0), (0, pad), (0, 0))).reshape(B, nc, CHUNK, GMLP_HEADS, GMLP_HEAD_DIM)
    causal = jnp.tril(jnp.ones((CHUNK, CHUNK), bool))
    wm = jnp.where(causal[None], ws.astype(f32), 0.0)
    mix = jnp.einsum('hij,bcjhd->bcihd', wm, vh) + bs.astype(f32).T[None, None, :, :, None]
    mix = mix.reshape(B, nc * CHUNK, GMLP_DIM)[:, :T]
    return u * mix, v


def rglru_mixer(q, conv_buf, h0, conv_w, conv_b, wx, bx, wa, ba, lam):
    f32 = jnp.float32
    B, T, _ = q.shape
    q = q.astype(f32)
    gate_in, xr = q[..., :LRU_DIM], q[..., LRU_DIM:]
    full = jnp.concatenate([conv_buf.astype(f32), xr], 1)
    xc = full[:, CONV_WIDTH - 1:] * conv_w[CONV_WIDTH - 1]
    for j in range(CONV_WIDTH - 1):
        xc = xc + full[:, j:j + T] * conv_w[j]
    xc = xc + conv_b
    xb = xc.reshape(B, T, LRU_BLOCKS, LRU_BLOCK_DIM)
    gx = jax.nn.sigmoid(jnp.einsum('btnc,ncd->btnd', xb, wx).reshape(B, T, LRU_DIM) + bx)
    ga = jax.nn.sigmoid(jnp.einsum('btnc,ncd->btnd', xb, wa).reshape(B, T, LRU_DIM) + ba)
    log_a = -LRU_C * ga * jax.nn.softplus(-lam)
    a = jnp.exp(log_a)
    b = jnp.sqrt(-jnp.expm1(2.0 * log_a)) * gx * xc
    b = b.at[:, 0].add(a[:, 0] * h0.astype(f32))
    comb = lambda l, r: (l[0] * r[0], r[0] * l[1] + r[1])
    _, h = lax.associative_scan(comb, (a, b), axis=1)
    y = h * jax.nn.gelu(gate_in)
    return y, full[:, -(CONV_WIDTH - 1):], h[:, -1]


def trunk(x, start, st_pool, st_shift, st_wkv, st_conv, st_lru, prm):
    n_pool, n_shift, n_wkv, n_conv, n_lru, n_v = [], [], [], [], [], []
    for layer in range(DEPTH):
        i = layer // 2
        if layer % 2 == 0:
            h = rmsnorm(x, prm['ev_norm_g'][i])
            p = h @ prm['ev_w_in'][i]
            yA, nb = pool_mixer(p[..., :POOL_DIM], st_pool[i], start, prm['pool_w'][i], prm['pool_scale'][i])
            yB, ns, nS = rwkv7_mixer(p[..., POOL_DIM:], st_shift[i], st_wkv[i], prm['rwkv_mu'][i], prm['rwkv_w0'][i],
                                     prm['rwkv_w_w2'][i], prm['rwkv_a0'][i], prm['rwkv_a_w2'][i], prm['rwkv_g_w2'][i],
                                     prm['rwkv_k_k'][i], prm['rwkv_k_a'][i], prm['rwkv_r_k'][i],
                                     prm['rwkv_gn_g'][i], prm['rwkv_gn_b'][i])
            y = jnp.concatenate([yA, yB], -1).astype(x.dtype)
            x = x + y @ prm['ev_w_out'][i]
            n_pool.append(nb); n_shift.append(ns); n_wkv.append(nS)
        else:
            h = rmsnorm(x, prm['od_norm_g'][i])
            q = h @ prm['od_w_in'][i]
            yC, vrows = gmlp_mixer(q[..., :2 * GMLP_DIM], prm['gmlp_ln_g'][i], prm['gmlp_ln_b'][i],
                                   prm['gmlp_ws'][i], prm['gmlp_bs'][i])
            yD, nc, nh = rglru_mixer(q[..., 2 * GMLP_DIM:], st_conv[i], st_lru[i], prm['lru_conv_w'][i],
                                     prm['lru_conv_b'][i], prm['lru_wx'][i], prm['lru_bx'][i], prm['lru_wa'][i],
                                     prm['lru_ba'][i], prm['lru_lam'][i])
            y = jnp.concatenate([yC, yD], -1).astype(x.dtype)
            x = x + y @ prm['od_w_out'][i]
            n_conv.append(nc); n_lru.append(nh); n_v.append(vrows)
        hf = rmsnorm(x, prm['ff_norm_g'][layer])
        x = x + jnp.square(jax.nn.relu(hf @ prm['ff_w1'][layer])) @ prm['ff_w2'][layer]
    y = rmsnorm(x, prm['final_norm_g'])
    return y, jnp.stack(n_pool), jnp.stack(n_shift), jnp.stack(n_wkv), jnp.stack(n_conv), jnp.stack(n_lru), jnp.stack(n_v)


def setup_inputs(seed: int = 0) -> dict:
    key = jax.random.key(seed)
    ks = iter(jax.random.split(key, 64))
    f32 = jnp.float32
    nrm = lambda shape, s: s * jax.random.normal(next(ks), shape, f32)
    gain = lambda shape: 1.0 + 0.02 * jax.random.normal(next(ks), shape, f32)
    unif = lambda shape, lo, hi: jax.random.uniform(next(ks), shape, f32, lo, hi)
    a_target = unif((N_ODD, LRU_DIM), 0.9, 0.999)
    s_root = a_target ** (1.0 / LRU_C)
    lru_lam = jnp.log(s_root) - jnp.log1p(-s_root)
    return {
        'x_prompt': nrm((BATCH, SEQ, D_MODEL), 1.0),
        'x_sample': nrm((DEC_BATCH, DEC_SEQ, D_MODEL), 1.0),
        'state_pool': nrm((N_EVEN, DEC_BATCH, POOL_BUF, POOL_DIM), 1.0),
        'state_shift': nrm((N_EVEN, DEC_BATCH, RWKV_PROJ), 1.0),
        'state_wkv': nrm((N_EVEN, DEC_BATCH, RWKV_HEADS, RWKV_HEAD_DIM, RWKV_HEAD_DIM), 1.0),
        'state_conv': nrm((N_ODD, DEC_BATCH, CONV_WIDTH - 1, LRU_DIM), 1.0),
        'state_lru': nrm((N_ODD, DEC_BATCH, LRU_DIM), 0.5),
        'ev_norm_g': gain((N_EVEN, D_MODEL)),
        'ev_w_in': nrm((N_EVEN, D_MODEL, EVEN_PROJ), D_MODEL ** -0.5),
        'pool_w': nrm((N_EVEN, POOL_GROUPS, POOL_GROUP_DIM, POOL_GROUP_DIM), POOL_GROUP_DIM ** -0.5),
        'pool_scale': unif((N_EVEN, POOL_DIM), 0.5, 1.0),
        'rwkv_mu': unif((N_EVEN, RWKV_PROJ), 0.0, 1.0),
        'rwkv_w0': unif((N_EVEN, RWKV_DIM), -6.0, 1.0),
        'rwkv_w_w2': nrm((N_EVEN, DECAY_LORA, RWKV_DIM), 0.5 * DECAY_LORA ** -0.5),
        'rwkv_a0': nrm((N_EVEN, RWKV_DIM), 0.5),
        'rwkv_a_w2': nrm((N_EVEN, AAA_LORA, RWKV_DIM), AAA_LORA ** -0.5),
        'rwkv_g_w2': nrm((N_EVEN, GATE_LORA, RWKV_DIM), GATE_LORA ** -0.5),
        'rwkv_k_k': unif((N_EVEN, RWKV_DIM), 0.7, 1.0),
        'rwkv_k_a': unif((N_EVEN, RWKV_DIM), 0.8, 1.2),
        'rwkv_r_k': nrm((N_EVEN, RWKV_HEADS, RWKV_HEAD_DIM), 0.1),
        'rwkv_gn_g': gain((N_EVEN, RWKV_DIM)),
        'rwkv_gn_b': nrm((N_EVEN, RWKV_DIM), 0.02),
        'ev_w_out': nrm((N_EVEN, EVEN_MIX, D_MODEL), EVEN_MIX ** -0.5),
        'od_norm_g': gain((N_ODD, D_MODEL)),
        'od_w_in': nrm((N_ODD, D_MODEL, ODD_PROJ), D_MODEL ** -0.5),
        'gmlp_ln_g': gain((N_ODD, GMLP_DIM)),
        'gmlp_ln_b': nrm((N_ODD, GMLP_DIM), 0.02),
        'gmlp_ws': nrm((N_ODD, GMLP_HEADS, CHUNK, CHUNK), CHUNK ** -0.5),
        'gmlp_bs': 1.0 + nrm((N_ODD, GMLP_HEADS, CHUNK), 0.01),
        'lru_conv_w': nrm((N_ODD, CONV_WIDTH, LRU_DIM), CONV_WIDTH ** -0.5),
        'lru_conv_b': nrm((N_ODD, LRU_DIM), 0.02),
        'lru_wx': nrm((N_ODD, LRU_BLOCKS, LRU_BLOCK_DIM, LRU_BLOCK_DIM), LRU_BLOCK_DIM ** -0.5),
        'lru_bx': nrm((N_ODD, LRU_DIM), 0.02),
        'lru_wa': nrm((N_ODD, LRU_BLOCKS, LRU_BLOCK_DIM, LRU_BLOCK_DIM), LRU_BLOCK_DIM ** -0.5),
        'lru_ba': nrm((N_ODD, LRU_DIM), 0.02),
        'lru_lam': lru_lam,
        'od_w_out': nrm((N_ODD, ODD_MIX, D_MODEL), ODD_MIX ** -0.5),
        'ff_norm_g': gain((DEPTH, D_MODEL)),
        'ff_w1': nrm((DEPTH, D_MODEL, D_FF), D_MODEL ** -0.5),
        'ff_w2': nrm((DEPTH, D_FF, D_MODEL), D_FF ** -0.5),
        'final_norm_g': gain((D_MODEL,)),
    }


def reference(x_prompt, x_sample, state_pool, state_shift, state_wkv, state_conv, state_lru,
              ev_norm_g, ev_w_in, pool_w, pool_scale, rwkv_mu, rwkv_w0, rwkv_w_w2, rwkv_a0, rwkv_a_w2,
              rwkv_g_w2, rwkv_k_k, rwkv_k_a, rwkv_r_k, rwkv_gn_g, rwkv_gn_b, ev_w_out,
              od_norm_g, od_w_in, gmlp_ln_g, gmlp_ln_b, gmlp_ws, gmlp_bs, lru_conv_w, lru_conv_b,
              lru_wx, lru_bx, lru_wa, lru_ba, lru_lam, od_w_out,
              ff_norm_g, ff_w1, ff_w2, final_norm_g):
    prm = dict(ev_norm_g=ev_norm_g, ev_w_in=ev_w_in, pool_w=pool_w, pool_scale=pool_scale, rwkv_mu=rwkv_mu,
               rwkv_w0=rwkv_w0, rwkv_w_w2=rwkv_w_w2, rwkv_a0=rwkv_a0, rwkv_a_w2=rwkv_a_w2, rwkv_g_w2=rwkv_g_w2,
               rwkv_k_k=rwkv_k_k, rwkv_k_a=rwkv_k_a, rwkv_r_k=rwkv_r_k, rwkv_gn_g=rwkv_gn_g, rwkv_gn_b=rwkv_gn_b,
               ev_w_out=ev_w_out, od_norm_g=od_norm_g, od_w_in=od_w_in, gmlp_ln_g=gmlp_ln_g, gmlp_ln_b=gmlp_ln_b,
               gmlp_ws=gmlp_ws, gmlp_bs=gmlp_bs, lru_conv_w=lru_conv_w, lru_conv_b=lru_conv_b, lru_wx=lru_wx,
               lru_bx=lru_bx, lru_wa=lru_wa, lru_ba=lru_ba, lru_lam=lru_lam, od_w_out=od_w_out,
               ff_norm_g=ff_norm_g, ff_w1=ff_w1, ff_w2=ff_w2, final_norm_g=final_norm_g)
    dt = x_prompt.dtype
    B = x_prompt.shape[0]
    z_pool = jnp.zeros((N_EVEN, B, POOL_BUF, POOL_DIM), dt)
    z_shift = jnp.zeros((N_EVEN, B, RWKV_PROJ), dt)
    z_wkv = jnp.zeros((N_EVEN, B, RWKV_HEADS, RWKV_HEAD_DIM, RWKV_HEAD_DIM), dt)
    z_conv = jnp.zeros((N_ODD, B, CONV_WIDTH - 1, LRU_DIM), dt)
    z_lru = jnp.zeros((N_ODD, B, LRU_DIM), dt)
    y_prompt, p_pool, p_shift, p_wkv, p_conv, p_lru, _ = trunk(
        x_prompt, 0, z_pool, z_shift, z_wkv, z_conv, z_lru, prm)
    y_sample, s_pool, s_shift, s_wkv, s_conv, s_lru, s_gmlp_v = trunk(
        x_sample, PAST_LEN, state_pool, state_shift, state_wkv, state_conv, state_lru, prm)
    return (y_prompt, y_sample, p_pool, p_shift, p_wkv, p_conv, p_lru,
            s_pool, s_shift, s_wkv, s_conv, s_lru, s_gmlp_v)
```

```python
import math
import numpy as np
import concourse.bass as bass
import concourse.mybir as mybir
from concourse.bass_utils import run_bass_kernel_spmd

F32 = mybir.dt.float32
BF16 = mybir.dt.bfloat16
F32R = mybir.dt.float32r
AF = mybir.ActivationFunctionType
ALU = mybir.AluOpType

NCORE = 8
D = 1024
SEQ = 2048
SB = 16
SL = 4
NS = SB * SL
NTOK = SEQ + NS
BLKC = 2
BN = 128 * BLKC
ACT_TAB = {AF.Exp: "e", AF.Ln: "e", AF.Sigmoid: "s", AF.Tanh: "s", AF.Gelu_apprx_tanh: "g", AF.Sqrt: "q"}
GRAN = 64
FILL_NS, FILL_FRAC, FILL_MIN, FILL_MAX = 156.0, 0.5, 400.0, 20
SCHED = True
FILLERS = True


class Region:
    __slots__ = ("w", "rs")

    def __init__(self):
        self.w = None
        self.rs = []


class Chan:
    def __init__(self, sem):
        self.sem = sem
        self.count = 0


class Op:
    __slots__ = ("eng", "fn", "deps", "signal", "ordinal", "chan", "chan_val", "waits", "ndma", "name", "tiled",
                 "deps_all", "cost", "idx", "succ", "nwait", "ready_t", "dma_t", "rows", "tab")


class Prog:
    ENGS = ("pe", "dve", "act", "pool", "sp")

    def __init__(self, nc):
        self.nc = nc
        self.ops = {e: [] for e in self.ENGS}
        self.sems = {}
        self._ctx = []
        for e in self.ENGS:
            self.sems[e] = self._sem("s_" + e)
        self.nchan = 0
        self.order = []
        self.filler = None
        self.nfill = 0

    def make_filler(self, fn, dep):
        o = Op()
        o.eng = "pe"
        o.fn = fn
        o.deps = [dep]
        o.deps_all = [dep]
        o.signal = False
        o.ordinal = 0
        o.chan = None
        o.ndma = 0
        o.name = "fill"
        o.waits = None
        o.tiled = False
        o.rows = None
        o.tab = None
        o.cost = FILL_NS
        o.dma_t = 0.0
        return o

    def schedule(self, window=512, lat_x=900.0, lat_s=60.0):
        order = self.order
        for o in order:
            o.succ = []
            o.nwait = len(o.deps_all)
            o.ready_t = 0.0
        for o in order:
            for d in o.deps_all:
                d.succ.append(o)
        eng_free = {e: 0.0 for e in self.ENGS}
        new_ops = {e: [] for e in self.ENGS}
        last_tab = [None]
        TABLOAD = 1300.0
        win = []
        nxt = 0
        n = len(order)
        done = 0
        while done < n:
            while len(win) < window and nxt < n:
                win.append(order[nxt])
                nxt += 1
            best = None
            best_st = 0.0
            bi = -1
            for i, o in enumerate(win):
                if o.nwait:
                    continue
                st = eng_free[o.eng]
                if o.ready_t > st:
                    st = o.ready_t
                if o.tab is not None and o.tab != last_tab[0]:
                    st += TABLOAD
                if best is None or st < best_st:
                    best, best_st, bi = o, st, i
            o = best
            win.pop(bi)
            done += 1
            if o.eng == "pe" and self.filler is not None:
                gap = best_st - eng_free["pe"]
                if gap > FILL_MIN:
                    nf = min(FILL_MAX, int(gap * FILL_FRAC / FILL_NS))
                    for _ in range(nf):
                        new_ops["pe"].append(self.filler())
                    self.nfill += nf
            new_ops[o.eng].append(o)
            if o.tab is not None:
                last_tab[0] = o.tab
            eng_free[o.eng] = best_st + o.cost
            fin = best_st + o.cost + o.dma_t
            for sx in o.succ:
                lat = lat_s if (sx.eng == o.eng and o.chan is None) else lat_x
                t = fin + lat
                if t > sx.ready_t:
                    sx.ready_t = t
                sx.nwait -= 1
        self.ops = new_ops
        self.est_ns = max(eng_free.values())

    def _sem(self, name):
        cm = self.nc.semaphore(name)
        s = cm.__enter__()
        self._ctx.append(cm)
        return s

    def chan(self):
        self.nchan += 1
        return Chan(self._sem("ch%d" % self.nchan))

    def sbuf(self, name, shape, dtype):
        cm = self.nc.sbuf_tensor(name, shape, dtype)
        t = cm.__enter__()
        self._ctx.append(cm)
        return t

    def psum(self, name, shape, dtype):
        cm = self.nc.psum_tensor(name, shape, dtype)
        t = cm.__enter__()
        self._ctx.append(cm)
        return t

    limit = None
    nrec = 0

    def op(self, eng, fn, reads=(), writes=(), chan=None, ndma=0, name="", tiled=False, cost=200.0, dma_t=0.0, rows=None):
        self.nrec += 1
        if self.limit is not None and self.nrec > self.limit:
            return None
        o = Op()
        o.eng = eng
        o.fn = fn
        o.signal = False
        o.ordinal = 0
        o.chan = chan
        o.ndma = ndma
        o.name = name
        o.waits = None
        o.tiled = tiled
        o.rows = rows
        o.tab = None
        deps = []
        for r in reads:
            if r.w is not None:
                deps.append(r.w)
        for r in writes:
            if r.w is not None:
                deps.append(r.w)
            deps.extend(r.rs)
        seen = set()
        dd = []
        da = []
        for d in deps:
            if id(d) in seen or d is o:
                continue
            seen.add(id(d))
            da.append(d)
            if eng == "pe" and d.eng == "pe" and d.chan is None:
                ra, rb = rows, d.rows
                if ra is None or rb is None or not (ra[0] + ra[1] <= rb[0] or rb[0] + rb[1] <= ra[0]):
                    continue
            dd.append(d)
        o.deps = dd
        o.deps_all = da
        o.cost = cost
        o.dma_t = dma_t
        o.idx = len(self.order)
        self.order.append(o)
        if chan is not None:
            chan.count += ndma
            o.chan_val = 16 * chan.count
        for r in reads:
            r.rs.append(o)
        for r in writes:
            r.w = o
            r.rs = []
        self.ops[eng].append(o)
        return o

    def emit(self, final_wait_ops=()):
        nc = self.nc
        if final_wait_ops:
            o = Op()
            o.eng = "sp"
            o.fn = None
            o.deps = list(final_wait_ops)
            o.signal = False
            o.chan = None
            o.ndma = 0
            o.name = "final"
            o.deps_all = list(final_wait_ops)
            o.tiled = False
            o.rows = None
            o.tab = None
            o.ordinal = 0
            o.waits = None
            self.ops["sp"].append(o)
        for e in self.ENGS:
            for o in self.ops[e]:
                for d in o.deps:
                    if d.chan is None:
                        d.signal = True
        for e in self.ENGS:
            k = 0
            for o in self.ops[e]:
                if o.chan is None and o.signal:
                    k += 1
                    o.ordinal = k
        for e in self.ENGS:
            seen = {}
            for o in self.ops[e]:
                need = {}
                for d in o.deps:
                    if d.chan is not None:
                        key, val, sem = id(d.chan), d.chan_val, d.chan.sem
                    else:
                        key, val, sem = d.eng, d.ordinal, self.sems[d.eng]
                    if seen.get(key, 0) >= val:
                        continue
                    if key not in need or need[key][1] < val:
                        need[key] = (sem, val)
                for key, (sem, val) in need.items():
                    seen[key] = val
                o.waits = list(need.values())
        engmap = {"pe": "tensor", "dve": "vector", "act": "scalar", "pool": "gpsimd", "sp": "sync"}
        with nc.Block() as block:
            for e in self.ENGS:
                ops = self.ops[e]
                if not ops:
                    continue
                sem_e = self.sems[e]

                def body(eng, ops=ops, sem_e=sem_e):
                    for o in ops:
                        for sem, val in o.waits:
                            eng.wait_ge(sem, val)
                        if o.fn is None:
                            continue
                        r = o.fn(eng)
                        if o.chan is not None:
                            if not isinstance(r, (list, tuple)):
                                r = [r]
                            assert len(r) == o.ndma, (o.name, len(r), o.ndma)
                            for ins in r:
                                ins.then_inc(o.chan.sem, 16)
                        elif o.signal:
                            if isinstance(r, (list, tuple)):
                                r = r[-1]
                            assert r is not None, o.name
                            r.then_inc(sem_e, 1)

                getattr(block, engmap[e])(body)

    def close(self):
        for cm in reversed(self._ctx):
            cm.__exit__(None, None, None)
        self._ctx = []


class Buf:
    def __init__(self, K, off, n, ten=None, gr=None):
        self.K = K
        self.off = off
        self.n = n
        self._chan = None
        self.ten = ten if ten is not None else K.arena
        self.gr = gr if gr is not None else K.gran

    def ap(self, c0=0, c1=None, p0=0, p1=128):
        if c1 is None:
            c1 = self.n
        return self.ten[p0:p1, self.off + c0:self.off + c1]

    def apb(self, c0=0, c1=None, p0=0, p1=128):
        if c1 is None:
            c1 = 2 * self.n
        return self.K.arena[p0:p1, self.off:self.off + self.n].bitcast(BF16)[:, c0:c1]

    def v3(self, a, b, p0=0, p1=128):
        return self.K.arena[p0:p1, self.off:self.off + a * b].rearrange("p (a b) -> p a b", b=b)

    def sub(self, c0, n):
        return Buf(self.K, self.off + c0, n)

    def regs(self):
        g0 = self.off // GRAN
        g1 = (self.off + self.n - 1) // GRAN
        return self.gr[g0:g1 + 1]

    @property
    def chan(self):
        if self._chan is None:
            self._chan = self.K.P.chan()
        return self._chan


class PS:
    def __init__(self, t, ncols):
        self.t = t
        self.n = ncols
        self.r = [Region()]

    def ap(self, c0=0, c1=None, p0=0, p1=128):
        if c1 is None:
            c1 = self.n
        return self.t[p0:p1, c0:c1]

    def regs(self):
        return self.r


class RegObj:
    def __init__(self):
        self.r = [Region()]

    def regs(self):
        return self.r


def _regs(lst):
    out = []
    for b in lst:
        out.extend(b.regs())
    return out


class KB:
    def __init__(self, nc, dr):
        self.nc = nc
        self.dr = dr
        self.P = Prog(nc)
        self.NA = 53184 - 4096 + 2048
        self.arena = self.P.sbuf("arena", [128, self.NA], F32)
        self.arena2 = self.P.sbuf("arena_r", [128, 4096], F32R)
        self.gran2 = [Region() for _ in range(4096 // GRAN + 1)]
        self.gran = [Region() for _ in range(self.NA // GRAN + 2)]
        self.top = 0
        self.psb = [PS(self.P.psum("psb%d" % i, [128, 512], F32), 512) for i in range(8)]
        self.pi = 0
        self.stores = []
        self.rr = {"ev": 0, "el": 0}

    def alloc(self, n):
        n = (n + GRAN - 1) // GRAN * GRAN
        b = Buf(self, self.top, n)
        self.top += n
        assert self.top <= self.NA, self.top
        return b

    def ps(self):
        p = self.psb[self.pi % 7]
        self.pi += 1
        return p

    def ev_eng(self):
        self.rr["ev"] += 1
        return "act" if self.rr["ev"] % 2 else "dve"

    def el_eng(self):
        self.rr["el"] += 1
        return "pool" if self.rr["el"] % 2 else "dve"

    def _op(self, eng, fn, R, W, name="", tiled=False, cost=200.0, rows=None):
        Rr = [b for b in R if not isinstance(b, PS)]
        Ww = list(W) + [b for b in R if isinstance(b, PS) and b not in W]
        return self.P.op(eng, fn, _regs(Rr), _regs(Ww), name=name, tiled=tiled, cost=cost, rows=rows)

    @staticmethod
    def _ec(eng, out):
        fs = out.free_size()
        if eng == "pool":
            return 250.0 + fs * 1.6
        if eng == "act":
            return 220.0 + fs * 0.75
        return 120.0 + fs * 1.05

    def tt(self, eng, out, in0, in1, op, R, W):
        return self._op(eng, lambda e: e.tensor_tensor(out=out, in0=in0, in1=in1, op=op), R, W, "tt", cost=self._ec(eng, out))

    def ts(self, eng, out, in0, s1, s2, op0, op1, R, W):
        if op1 is None:
            return self._op(eng, lambda e: e.tensor_scalar(out=out, in0=in0, scalar1=s1, scalar2=None, op0=op0), R, W, "ts", cost=self._ec(eng, out))
        return self._op(eng, lambda e: e.tensor_scalar(out=out, in0=in0, scalar1=s1, scalar2=s2, op0=op0, op1=op1), R, W, "ts", cost=self._ec(eng, out))

    def stt(self, out, in0, sc, in1, op0, op1, R, W):
        return self._op("dve", lambda e: e.scalar_tensor_tensor(out=out, in0=in0, scalar=sc, in1=in1, op0=op0, op1=op1), R, W, "stt", cost=self._ec("dve", out) + out.free_size() * 1.0)

    def act(self, out, in_, func, R, W, bias=None, scale=None):
        kw = {}
        if bias is not None:
            kw["bias"] = bias
        if scale is not None:
            kw["scale"] = scale
        o = self._op("act", lambda e: e.activation(out=out, in_=in_, func=func, **kw), R, W, "act", cost=self._ec("act", out) + 90.0 * len(kw))
        if o is not None:
            o.tab = ACT_TAB.get(func)
        return o

    def cp(self, eng, out, in_, R, W):
        if eng == "act":
            return self.act(out, in_, AF.Copy, R, W)
        return self._op(eng, lambda e: e.tensor_copy(out=out, in_=in_), R, W, "cp", cost=self._ec(eng, out))

    def memset(self, eng, out, val, W):
        return self._op(eng, lambda e: e.memset(out, val), [], W, "memset", cost=self._ec(eng, out))

    def mm(self, out, lhsT, rhs, start, stop, R, W):
        rows = (lhsT.base_partition(), lhsT.partition_size())
        passes = 4.0 if lhsT.dtype == F32 else (2.0 if lhsT.dtype == F32R else 1.0)
        cost = 40.0 + max(64.0, out.free_size() * passes) / 2.2
        return self._op("pe", lambda e: e.matmul(out, lhsT, rhs, start=start, stop=stop), R, W, "mm", cost=cost, rows=rows)

    def tr(self, out, in_, ident, R, W):
        return self._op("pe", lambda e: e.transpose(out, in_, ident), R, W, "tr", cost=60.0 + out.free_size() * 2.0 / 2.2,
                        rows=(in_.base_partition(), in_.partition_size()))

    def scan(self, out, d0, d1, init, R, W):
        return self._op("dve", lambda e: e.tensor_tensor_scan(out=out, data0=d0, data1=d1, initial=init, op0=ALU.mult, op1=ALU.add), R, W, "scan", cost=150.0 + out.free_size() * 2.1)

    def load(self, buf, out, in_, R=()):
        return self.P.op("sp", lambda e: e.dma_start(out=out, in_=in_), _regs(R), _regs([buf]), chan=buf.chan, ndma=1, name="load",
                         cost=120.0, dma_t=2000.0 + out.partition_size() * out.free_size() * 4 / 180.0)

    def store(self, buf, out, in_, W=()):
        o = self.P.op("sp", lambda e: e.dma_start(out=out, in_=in_), _regs([buf]), _regs(W), chan=buf.chan, ndma=1, name="store",
                      cost=120.0, dma_t=2000.0 + in_.partition_size() * in_.free_size() * 4 / 180.0)
        if o is not None:
            self.stores.append(o)
        return o


def build_program(dbg=False):
    nc = bass.Bass("TRN2", target_bir_lowering=False, dynamic_dma_scratch_size=8192)
    shapes_in = {
        "xin": [NTOK, D], "st_pool": [SB * 15, 256], "st_shift": [SB, 2560], "st_wkv": [SB * 12 * 64, 64],
        "st_conv": [SB * 3, 512], "st_lru": [SB, 512],
        "ev_w_in": [D, 2816], "ev_w_out": [D, D], "od_w_in": [D, 2048], "od_w_out": [D, D],
        "ff_w1": [2, D, 4096], "ff_w2": [2, 4096, D],
    }
    for nm, ncol in CONST_COLS.items():
        shapes_in[nm] = [128, ncol]
    shapes_out = {
        "y": [NTOK, D], "o_pool_p": [15, 256], "o_shift_p": [1, 2560], "o_wkv_p": [768, 64], "o_conv_p": [3, 512],
        "o_lru_p": [1, 512], "o_pool_s": [SB * 15, 256], "o_shift_s": [SB, 2560], "o_wkv_s": [SB * 768, 64],
        "o_conv_s": [SB * 3, 512], "o_lru_s": [SB, 512], "o_gv_s": [NS, 512],
    }
    dr = {}
    for nm, sh in shapes_in.items():
        dr[nm] = nc.dram_tensor(nm, sh, F32, kind="ExternalInput").ap()
    for nm, sh in shapes_out.items():
        dr[nm] = nc.dram_tensor(nm, sh, F32, kind="ExternalOutput").ap()
    dr["xscr"] = nc.dram_tensor("xscr", [8, 128, NTOK], F32, kind="Internal").ap()
    K = KB(nc, dr)
    _emit_all(K)
    if SCHED:
        K.P.schedule()
    K.P.emit(final_wait_ops=K.stores)
    K.P.close()
    return nc


CONST_COLS = {
    "pcols": 0,
    "ident": 128, "bd1": 128, "ones": 128,
    "mN_p": 128, "m2_p": 256, "mN_s": 64, "m2_s": 128, "rm_p": 128, "rm_s": 64, "rowmask": 16,
    "invc0": 256, "invc1": 256, "wa_w2": 768, "g_w2": 768, "poolw_bd": 256, "wx_bd": 512, "wa_bd": 512,
    "wmT": 512, "bs_bc": 512, "wsb": 64,
}
PC = {}


def _pc_layout():
    names = [("ev_norm_g", 8), ("pool_scale", 2), ("mu", 20), ("w0", 6), ("a0", 6), ("k_k", 6), ("k_a", 6), ("r_k", 6),
             ("gn_g", 6), ("gn_b", 6), ("od_norm_g", 8), ("ln_g", 4), ("ln_b", 4), ("conv_w0", 4), ("conv_w1", 4),
             ("conv_w2", 4), ("conv_w3", 4), ("conv_b", 4), ("bx", 4), ("ba", 4), ("lam", 4), ("ff_g0", 8), ("ff_g1", 8),
             ("fin_g", 8), ("c_eps6", 1), ("c_epsgn", 1), ("c_epsln", 1), ("c_one", 1), ("c_tiny", 1)]
    off = 0
    for nm, n in names:
        PC[nm] = (off, n)
        off += n
    CONST_COLS["pcols"] = off


_pc_layout()


def _emit_all(K):
    P = K.P
    dr = K.dr
    C = {}
    for nm, ncol in CONST_COLS.items():
        C[nm] = K.alloc(ncol)
    first = True
    for nm in CONST_COLS:
        K.load(C[nm], C[nm].ap(0, CONST_COLS[nm]), dr[nm])

    def col(nm, j=0):
        o, n = PC[nm]
        return C["pcols"].ap(o + j, o + j + 1)

    ident = C["ident"]
    bd1 = C["bd1"]
    ones = C["ones"]
    if FILLERS:
        fsrc = K.alloc(192)
        K.cp("dve", fsrc.apb(0, 128), ones.ap(0, 128), [ones], [fsrc])
        K.cp("dve", fsrc.apb(128, 256), ones.ap(0, 128), [ones], [fsrc])
        fdep = K.cp("dve", fsrc.apb(256, 384), ident.ap(0, 128), [ident], [fsrc])
        f_out, f_l, f_r = K.psb[7].ap(0, 256), fsrc.apb(0, 128), fsrc.apb(128, 384)
        K.P.filler = lambda: K.P.make_filler(lambda e: e.matmul(f_out, f_l, f_r, start=True, stop=True), fdep)

    for h in range(4):
        K.tt("dve", C["wmT"].ap(h * 128, h * 128 + 128), C["wmT"].ap(h * 128, h * 128 + 128), C["m2_p"].ap(128, 256), ALU.mult,
             [C["wmT"], C["m2_p"]], [C["wmT"]])
    nsp8 = K.alloc(4)
    o_l, _ = PC["lam"]
    K.act(nsp8.ap(0, 4), C["pcols"].ap(o_l, o_l + 4), AF.Exp, [C["pcols"]], [nsp8], scale=-1.0)
    K.act(nsp8.ap(0, 4), nsp8.ap(0, 4), AF.Ln, [nsp8], [nsp8], bias=col("c_one"))
    K.ts("dve", nsp8.ap(0, 4), nsp8.ap(0, 4), -8.0, None, ALU.mult, None, [nsp8], [nsp8])

    EP = [K.alloc(304) for _ in range(2)]
    ES = K.alloc(20 * 129)
    EC = [K.alloc(3 + BN) for _ in range(4)]
    HL = [K.alloc(16) for _ in range(4)]
    Hp = K.alloc(6 * 64)
    for b in EP + EC + HL + [ES, Hp]:
        K.memset("pool", b.ap(), 0.0, [b])

    NWS = 6
    WB = [K.alloc(8 * 128) for _ in range(NWS)]

    def wtile(nk, cw, src3):
        i = wsl[0] % NWS
        wsl[0] += 1
        wb = WB[i]
        dst = wb.apb(0, nk * cw).rearrange("p (k c) -> p k c", c=cw)
        K.P.op("pool", lambda e: e.dma_start(out=dst, in_=src3), [], _regs([wb]), chan=wb.chan, ndma=1, name="wload",
               cost=1000.0, dma_t=2500.0 + 128 * nk * cw * 4 / 160.0)
        return wb

    stg_in = [K.alloc(1024) for _ in range(1)]
    stg_out = [K.alloc(1024) for _ in range(1)]
    hb = [K.alloc(BN // 2) for _ in range(8)]
    ym = [K.alloc(BN // 2) for _ in range(8)]
    pr = [K.alloc(BN) for _ in range(22)]
    base_top = K.top
    xb = [K.alloc(BN) for _ in range(8)]
    regA = [hb[0].off, base_top]

    def allocA(n):
        n = (n + GRAN - 1) // GRAN * GRAN
        if regA[0] + n <= regA[1]:
            b = Buf(K, regA[0], n)
            regA[0] += n
            return b
        return K.alloc(n)

    wsl = [0]

    def wslot():
        b = WB[wsl[0] % 2]
        wsl[0] += 1
        return b

    so = [0]

    def sout():
        b = stg_out[0]
        so[0] += 1
        return b

    si = [0]

    def sin():
        b = stg_in[0]
        si[0] += 1
        return b

    def fm2tm_store(srcs, m, dst_ap):
        k = len(srcs)
        st = sout()
        for g0 in range(0, k, 4):
            g = srcs[g0:g0 + 4]
            ps = K.ps()
            for i, (a, bufs) in enumerate(g):
                K.tr(ps.ap(i * 128, i * 128 + 128, 0, m), a, ident.ap(0, 128), bufs + [ident], [ps])
            K.cp(K.ev_eng(), st.ap(g0 * 128, (g0 + len(g)) * 128, 0, m), ps.ap(0, len(g) * 128, 0, m), [ps], [st])
        K.store(st, dst_ap, st.ap(0, k * 128, 0, m))

    def tm2fm_load(src_ap, m, k, dsts):
        st = sin()
        K.load(st, st.ap(0, k * 128, 0, m), src_ap)
        for g0 in range(0, k, 4):
            g = dsts[g0:g0 + 4]
            ps = K.ps()
            for i in range(len(g)):
                K.tr(ps.ap(i * m, i * m + m), st.ap((g0 + i) * 128, (g0 + i + 1) * 128, 0, m), ident.ap(0, m, 0, m), [st, ident], [ps])
            for i, (oa, bufs, shp) in enumerate(g):
                src = ps.ap(i * m, i * m + m)
                if shp is not None:
                    src = src.rearrange("p (a b) -> p a b", b=shp)
                K.cp(K.ev_eng(), oa, src, [ps], bufs)

    def rmsnorm(src, gname, dst, n, epsname="c_eps6", bf=True):
        m0 = K.top
        sq = [K.alloc(n) for _ in range(1)]
        rs = K.alloc(n)
        ps = K.ps()
        for k in range(8):
            s = sq[0]
            K.act(s.ap(0, n), src[k].ap(0, n), AF.Square, [src[k]], [s])
            K.mm(ps.ap(0, n), ones.ap(0, 128), s.ap(0, n), k == 0, k == 7, [ones, s], [ps])
        K.act(rs.ap(0, n), ps.ap(0, n), AF.Ln, [ps], [rs], bias=col(epsname), scale=1.0 / D)
        K.act(rs.ap(0, n), rs.ap(0, n), AF.Exp, [rs], [rs], scale=-0.5)
        for k in range(8):
            K.stt(dst[k].apb(0, n) if bf else dst[k].ap(0, n), src[k].ap(0, n), col(gname, k), rs.ap(0, n), ALU.mult, ALU.mult, [src[k], rs, C["pcols"]], [dst[k]])
        K.top = m0

    def linear(wname, lidx, K_tiles, ncols_out, n, sink):
        w = dr[wname]
        if lidx is not None:
            w = w[lidx]
        nk = len(K_tiles)
        for c0 in range(0, ncols_out, 256):
            cw = min(256, ncols_out - c0)
            wb = wtile(nk, cw, w[:, c0:c0 + cw].rearrange("(k p) c -> p k c", p=128))
            for mi in range(cw // 128):
                ps = K.ps()
                for k in range(nk):
                    a, bufs = K_tiles[k]
                    K.mm(ps.ap(0, n), wb.apb(k * cw + mi * 128, k * cw + mi * 128 + 128), a, k == 0, k == nk - 1, [wb] + bufs, [ps])
                sink(c0 // 128 + mi, ps)

    TBS = [(t0, min(512, NTOK - t0)) for t0 in range(0, NTOK, 512)]

    def ffn_all(layer, xall):
        m0 = K.top
        regA[0] = hb[0].off
        hf = [allocA(NTOK // 2) for _ in range(8)]
        acs = [[allocA(NTOK // 2) for _ in range(2)] for _ in range(1)]
        tmp = [allocA(512) for _ in range(2)]
        for (t0, tn) in TBS:
            rmsnorm([xall[k].sub(t0, tn) for k in range(8)], "ff_g%d" % layer, [hf[k].sub(t0 // 2, tn // 2) for k in range(8)], tn)
        w1 = dr["ff_w1"][layer]
        w2 = dr["ff_w2"][layer]
        ti = 0
        for c in range(16):
            ac = acs[0]
            wb1 = wtile(8, 256, w1[:, c * 256:(c + 1) * 256].rearrange("(k p) c -> p k c", p=128))
            for mi in range(2):
                for (t0, tn) in TBS:
                    ps = K.ps()
                    for k in range(8):
                        hk = hf[k].sub(t0 // 2, tn // 2)
                        K.mm(ps.ap(0, tn), wb1.apb(k * 256 + mi * 128, k * 256 + mi * 128 + 128), hk.apb(0, tn), k == 0, k == 7, [wb1, hk], [ps])
                    t = tmp[ti % 2]
                    ti += 1
                    K.act(t.ap(0, tn), ps.ap(0, tn), AF.Relu, [ps], [t])
                    ak = ac[mi].sub(t0 // 2, tn // 2)
                    K.tt("pool", ak.apb(0, tn), t.ap(0, tn), t.ap(0, tn), ALU.mult, [t], [ak])
            wb2 = wtile(2, 1024, w2[c * 256:(c + 1) * 256, :].rearrange("(k p) c -> p k c", p=128))
            for m in range(8):
                for (t0, tn) in TBS:
                    ps = K.ps()
                    for k in range(2):
                        ak = ac[k].sub(t0 // 2, tn // 2)
                        K.mm(ps.ap(0, tn), wb2.apb(k * 1024 + m * 128, k * 1024 + m * 128 + 128), ak.apb(0, tn), k == 0, k == 1, [wb2, ak], [ps])
                    xm = xall[m].sub(t0, tn)
                    K.tt("dve", xm.ap(0, tn), xm.ap(0, tn), ps.ap(0, tn), ALU.add, [xm, ps], [xm])
        K.top = m0

    def v3(buf, nseq, width, c0, ln, p0=0, p1=128):
        return K.arena[p0:p1, buf.off:buf.off + nseq * width].rearrange("p (s w) -> p s w", w=width)[:, :, c0:c0 + ln]

    def pool_mixer(ch, o):
        n, nseq, L = ch["n"], ch["nseq"], ch["L"]
        W = 15 + L
        m0 = K.top
        S = [K.alloc(304) for _ in range(4)]
        Dd = K.alloc(128)
        invc = C["invc0"] if (ch["first"] and not ch["sample"]) else C["invc1"]
        for j in range(2):
            e = EP[j]
            if not ch["first"]:
                K.cp("pool", v3(e, nseq, W, 0, 15), v3(e, nseq, W, L, 15), [e], [e])
            K.cp("act", v3(e, nseq, W, 15, L), pr[j].ap(o, o + n).rearrange("p (s l) -> p s l", l=L), [pr[j]], [e])
            prev = e
            shifts = [1, 2, 4, 8] if j == 1 else [1, 2]
            lo = 0
            for si_, sh in enumerate(shifts):
                lo2 = lo + sh
                K.tt("dve" if si_ % 2 == 0 else "pool", v3(S[si_], nseq, W, lo2, W - lo2), v3(prev, nseq, W, lo2, W - lo2), v3(prev, nseq, W, lo, W - lo2), ALU.add,
                     [prev], [S[si_]])
                prev = S[si_]
                lo = lo2
            slo, shi = (S[0], S[1]) if j == 0 else (S[2], S[3])
            dv = Dd.ap(0, n).rearrange("p (s l) -> p s l", l=L)
            for (p0, p1, sbuf_) in ((0, 64, slo), (64, 128, shi)):
                K.tt("dve", Dd.ap(0, n, p0, p1).rearrange("p (s l) -> p s l", l=L), v3(sbuf_, nseq, W, 15, L, p0, p1),
                     invc.ap(j * 128, j * 128 + n, p0, p1).rearrange("p (s l) -> p s l", l=L), ALU.mult, [sbuf_, invc], [Dd])
            K.tt("dve", dv, dv, v3(e, nseq, W, 15, L), ALU.subtract, [Dd, e], [Dd])
            ps = K.ps()
            K.mm(ps.ap(0, n), C["poolw_bd"].ap(j * 128, j * 128 + 128), Dd.ap(0, n), True, True, [C["poolw_bd"], Dd], [ps])
            K.ts("dve", ym[j].apb(o, o + n), ps.ap(0, n), col("pool_scale", j), None, ALU.mult, None, [ps, C["pcols"]], [ym[j]])
        if ch["last"]:
            stc = [K.alloc(256) for _ in range(2)]
            for j in range(2):
                K.cp("pool", stc[j].ap(0, nseq * 15).rearrange("p (s l) -> p s l", l=15), v3(EP[j], nseq, W, L, 15), [EP[j]], [stc[j]])
            dst = dr["o_pool_s"] if ch["sample"] else dr["o_pool_p"]
            tot = nseq * 15
            piece = 120 if tot > 128 else tot
            for r0 in range(0, tot, piece):
                fm2tm_store([(stc[j].ap(r0, r0 + piece), [stc[j]]) for j in range(2)], piece, dst[r0:r0 + piece, :])
        K.top = m0

    def rwkv(ch, o):
        n, nseq, L = ch["n"], ch["nseq"], ch["L"]
        sample = ch["sample"]
        Wd = 1 + L
        m0 = K.top
        XS = [K.alloc(128) for _ in range(20)]
        mN = C["mN_s"] if sample else C["mN_p"]
        m2 = C["m2_s"] if sample else C["m2_p"]
        rm = C["rm_s"] if sample else C["rm_p"]
        ESq = [ES.sub(q * 129, 129) for q in range(20)]
        esv = lambda q, c0, ln: K.arena[:, ES.off + q * 129: ES.off + q * 129 + nseq * Wd].rearrange("p (s w) -> p s w", w=Wd)[:, :, c0:c0 + ln]
        dtmp = [K.alloc(128) for _ in range(8)]
        for q in range(20):
            if not ch["first"]:
                K.cp("pool", esv(q, 0, 1), esv(q, L, 1), [ESq[q]], [ESq[q]])
        for q in range(20):
            K.cp("act" if q % 2 else "pool", esv(q, 1, L), pr[2 + q].ap(o, o + n).rearrange("p (s l) -> p s l", l=L), [pr[2 + q]], [ESq[q]])
        for q in range(20):
            d = dtmp[q % 8]
            dv = d.ap(0, n).rearrange("p (s l) -> p s l", l=L)
            K.tt("pool" if q % 2 else "dve", dv, esv(q, 0, L), esv(q, 1, L), ALU.subtract, [ESq[q]], [d])
            K.stt(XS[q].ap(0, n).rearrange("p (s l) -> p s l", l=L), dv, col("mu", q), esv(q, 1, L), ALU.mult, ALU.add, [d, ESq[q], C["pcols"]], [XS[q]])
        if ch["last"]:
            stc = K.alloc(20 * 16)
            K.cp("pool", stc.ap(0, 20 * nseq).rearrange("p (q s) -> p q s", s=nseq),
                 K.arena[:, ES.off:ES.off + 20 * 129].rearrange("p (q w) -> p q w", w=129)[:, :, 0:nseq * Wd].rearrange("p q (s w) -> p q s w", w=Wd)[:, :, :, L],
                 [ES], [stc])
            dst = dr["o_shift_s"] if sample else dr["o_shift_p"]
            for g0 in range(0, 20, 8):
                g = list(range(g0, min(20, g0 + 8)))
                fm2tm_store([(stc.ap(q * nseq, q * nseq + nseq), [stc]) for q in g], nseq, dst[:, g0 * 128:(g0 + len(g)) * 128])
        Rr = XS[0:6]
        Kk = XS[6:12]
        Vv = XS[12:18]
        TH = K.alloc(128)
        SGg = K.alloc(128)
        K.act(TH.ap(0, n, 0, 64), XS[18].ap(0, n, 0, 64), AF.Tanh, [XS[18]], [TH])
        K.act(SGg.ap(0, n), XS[19].ap(0, n), AF.Sigmoid, [XS[19]], [SGg])
        LW = [K.alloc(128) for _ in range(6)]
        Aa = [K.alloc(128) for _ in range(6)]
        Gg = [K.alloc(128) for _ in range(6)]
        for j in range(6):
            ps = K.ps()
            K.mm(ps.ap(0, n), C["wa_w2"].ap(j * 128, j * 128 + 128, 0, 64), TH.ap(0, n, 0, 64), True, True, [C["wa_w2"], TH], [ps])
            K.act(LW[j].ap(0, n), ps.ap(0, n), AF.Sigmoid, [ps, C["pcols"]], [LW[j]], bias=col("w0", j))
            ps = K.ps()
            K.mm(ps.ap(0, n), C["wa_w2"].ap(j * 128, j * 128 + 128, 64, 128), XS[18].ap(0, n, 64, 128), True, True, [C["wa_w2"], XS[18]], [ps])
            K.act(Aa[j].ap(0, n), ps.ap(0, n), AF.Sigmoid, [ps, C["pcols"]], [Aa[j]], bias=col("a0", j))
            ps = K.ps()
            K.mm(ps.ap(0, n), C["g_w2"].ap(j * 128, j * 128 + 128), SGg.ap(0, n), True, True, [C["g_w2"], SGg], [ps])
            K.cp("dve", Gg[j].ap(0, n), ps.ap(0, n), [ps], [Gg[j]])
        KK = [K.alloc(128) for _ in range(6)]
        KP = [K.alloc(128) for _ in range(6)]
        BON = [K.alloc(128) for _ in range(6)]
        t1 = [K.alloc(128) for _ in range(6)]
        t2 = [K.alloc(128) for _ in range(6)]
        t3 = [K.alloc(128) for _ in range(6)]
        for j in range(6):
            a, b = t1[j], t2[j]
            K.ts("pool", KK[j].ap(0, n), Kk[j].ap(0, n), col("k_k", j), None, ALU.mult, None, [Kk[j], C["pcols"]], [KK[j]])
            K.tt("pool", a.ap(0, n), KK[j].ap(0, n), KK[j].ap(0, n), ALU.mult, [KK[j]], [a])
            ps = K.ps()
            K.mm(ps.ap(0, n), bd1.ap(0, 128), a.ap(0, n), True, True, [bd1, a], [ps])
            K.act(b.ap(0, n), ps.ap(0, n), AF.Ln, [ps, C["pcols"]], [b], bias=col("c_tiny"))
            K.act(b.ap(0, n), b.ap(0, n), AF.Exp, [b], [b], scale=-0.5)
            K.tt("dve", KK[j].ap(0, n), KK[j].ap(0, n), b.ap(0, n), ALU.mult, [KK[j], b], [KK[j]])
            K.ts("dve", a.ap(0, n), Aa[j].ap(0, n), -1.0, col("k_a", j), ALU.add, ALU.mult, [Aa[j], C["pcols"]], [a])
            K.stt(KP[j].ap(0, n), a.ap(0, n), 1.0, Kk[j].ap(0, n), ALU.add, ALU.mult, [a, Kk[j]], [KP[j]])
            K.stt(a.ap(0, n), Rr[j].ap(0, n), col("r_k", j), KP[j].ap(0, n), ALU.mult, ALU.mult, [Rr[j], KP[j], C["pcols"]], [a])
            ps = K.ps()
            K.mm(ps.ap(0, n), bd1.ap(0, 128), a.ap(0, n), True, True, [bd1, a], [ps])
            K.tt("dve", BON[j].ap(0, n), ps.ap(0, n), Vv[j].ap(0, n), ALU.mult, [ps, Vv[j]], [BON[j]])
            K.ts("pool", BON[j].ap(0, n), BON[j].ap(0, n), col("gn_b", j), None, ALU.add, None, [BON[j], C["pcols"]], [BON[j]])
        LP = [K.alloc(128) for _ in range(6)]
        AR = [K.alloc(256) for _ in range(6)]
        BT = KK
        KT = Kk
        PCc = [K.alloc(16) for _ in range(6)]
        for j in range(6):
            e = t1[j]
            CD = -0.6065306597126334
            K.scan(LP[j].ap(0, n), rm.ap(0, n), LW[j].ap(0, n), 0.0, [rm, LW[j]], [LP[j]])
            K.act(e.ap(0, n), LP[j].ap(0, n), AF.Exp, [LP[j]], [e], scale=CD)
            K.tt("pool", AR[j].ap(n, 2 * n), Rr[j].ap(0, n), e.ap(0, n), ALU.mult, [Rr[j], e], [AR[j]])
            K.act(PCc[j].ap(0, nseq), LP[j].ap(0, n).rearrange("p (s l) -> p s l", l=L)[:, :, L - 1], AF.Exp, [LP[j]], [PCc[j]], scale=CD)
            e2 = t2[j]
            K.tt("dve", e2.ap(0, n), LP[j].ap(0, n), LW[j].ap(0, n), ALU.subtract, [LP[j], LW[j]], [e2])
            K.act(e2.ap(0, n), e2.ap(0, n), AF.Exp, [e2], [e2], scale=CD)
            K.stt(AR[j].ap(0, n), KK[j].ap(0, n), -1.0, e2.ap(0, n), ALU.mult, ALU.mult, [KK[j], e2], [AR[j]])
            e3 = t3[j]
            K.act(e3.ap(0, n), LP[j].ap(0, n), AF.Exp, [LP[j]], [e3], scale=-CD)
            K.tt("pool", BT[j].ap(0, n), KK[j].ap(0, n), Aa[j].ap(0, n), ALU.mult, [KK[j], Aa[j]], [BT[j]])
            K.tt("dve", BT[j].ap(0, n), BT[j].ap(0, n), e3.ap(0, n), ALU.mult, [BT[j], e3], [BT[j]])
            K.tt("pool", KT[j].ap(0, n), KP[j].ap(0, n), e3.ap(0, n), ALU.mult, [KP[j], e3], [KT[j]])
        TM = [K.alloc(768) for _ in range(3)]
        for qi, srcl in enumerate((Vv, BT, KT)):
            for g0 in (0, 4):
                g = list(range(g0, min(6, g0 + 4)))
                ps = K.ps()
                for i, j in enumerate(g):
                    K.tr(ps.ap(i * 128, i * 128 + 128, 0, n), srcl[j].ap(0, n), ident.ap(0, 128), [srcl[j], ident], [ps])
                K.cp(K.ev_eng(), TM[qi].ap(g0 * 128, (g0 + len(g)) * 128, 0, n), ps.ap(0, len(g) * 128, 0, n), [ps], [TM[qi]])
        Vtm, Btm, Ktm = TM
        if sample:
            assert nseq == 16
            assert WB[NWS - 1].off + WB[NWS - 1].n - WB[0].off == 16 * 384
            Hs = [Buf(K, WB[0].off + s * 384, 384) for s in range(16)]
            stw = dr["st_wkv"].rearrange("(s h v) k -> s v h k", h=12, v=64)
            for s in range(nseq):
                st = sin()
                K.load(st, st.ap(0, 768, 0, 64).rearrange("p (h k) -> p h k", k=64), stw[s])
                ps = K.ps()
                for j in range(6):
                    K.tr(ps.ap(j * 64, j * 64 + 64), st.ap(j * 128, j * 128 + 128, 0, 64), ident.ap(0, 64, 0, 64), [st, ident], [ps])
                K.cp(K.ev_eng(), Hs[s].ap(0, 384), ps.ap(0, 384), [ps], [Hs[s]])
        else:
            Hs = [Hp]
        OT = Rr
        Ut = K.alloc(768)
        nlev = int(math.ceil(math.log2(L)))
        GN = K.alloc(4 * n)
        G2 = K.alloc(8 * n)
        G3 = K.alloc(8 * n)
        if sample:
            Pb = [K.alloc(4 * n) for _ in range(2)]
            Ptb = [K.alloc(4 * n) for _ in range(2)]
            Tt = K.alloc(4 * n)
        else:
            PTb = [Buf(K, i_ * 1024, 1024, ten=K.arena2, gr=K.gran2) for i_ in range(2)]
            QTb = [Buf(K, 2048 + i_ * 1024, 1024, ten=K.arena2, gr=K.gran2) for i_ in range(2)]
        r32 = lambda b_, c0_, c1_: b_.ten[:, b_.off + c0_:b_.off + c1_]
        slot = lambda b_, h0_, nh_, w_: b_.ten[:, b_.off + h0_ * 256:b_.off + (h0_ + nh_) * 256].rearrange("p (h c) -> p h c", c=256)[:, :, w_ * 128:(w_ + 1) * 128]
        X0T = K.alloc(256)
        X0 = K.alloc(256)
        Xs = K.alloc(256)
        UVms = [K.alloc(512) for _ in range(1)] if sample else None
        uvi = [0]
        for hg in range(3):
            heads = [4 * hg + i for i in range(4)]
            psAe, psAo = K.ps(), K.ps()
            psB1, psB2 = K.ps(), K.ps()
            psC1, psC2 = K.ps(), K.ps()
            for hh, h in enumerate(heads):
                j, b = h // 2, 64 * (h % 2)
                aT = AR[j].ap(0, n, b, b + 64)
                arT = AR[j].ap(0, 2 * n, b, b + 64)
                bT = BT[j].ap(0, n, b, b + 64)
                kT = KT[j].ap(0, n, b, b + 64)
                psA = psAe if hh % 2 == 0 else psAo
                K.mm(psA.ap(hh * n, hh * n + n, 0, n), aT, bT, True, True, [AR[j], BT[j]], [psA])
                pB = psB1 if hh % 2 == 0 else psB2
                pC = psC1 if hh % 2 == 0 else psC2
                c0 = (hh // 2) * 2 * n
                K.mm(pB.ap(c0, c0 + 2 * n, 0, n), bT, arT, True, True, [AR[j], BT[j]], [pB])
                K.mm(pC.ap(c0, c0 + 2 * n, 0, n), kT, arT, True, True, [AR[j], KT[j]], [pC])
            for hh in range(4):
                psA = psAe if hh % 2 == 0 else psAo
                K.tt("dve", GN.ap(hh * n, hh * n + n, 0, n), psA.ap(hh * n, hh * n + n, 0, n), mN.ap(0, n, 0, n), ALU.mult, [psA, mN], [GN])
                pB = psB1 if hh % 2 == 0 else psB2
                pC = psC1 if hh % 2 == 0 else psC2
                c0 = (hh // 2) * 2 * n
                K.tt("dve", G2.ap(hh * 2 * n, hh * 2 * n + 2 * n, 0, n), pB.ap(c0, c0 + 2 * n, 0, n), m2.ap(0, 2 * n, 0, n), ALU.mult, [pB, m2], [G2])
                K.tt("dve", G3.ap(hh * 2 * n, hh * 2 * n + 2 * n, 0, n), pC.ap(c0, c0 + 2 * n, 0, n), m2.ap(0, 2 * n, 0, n), ALU.mult, [pC, m2], [G3])
            if sample:
                Pc, Ptc = Pb[0], Ptb[0]
                for hh in range(4):
                    K.cp("pool", Pc.ap(hh * n, hh * n + n, 0, n), GN.ap(hh * n, hh * n + n, 0, n), [GN], [Pc])
                    K.cp("pool", Ptc.ap(hh * n, hh * n + n, 0, n), G2.ap(hh * 2 * n, hh * 2 * n + n, 0, n), [G2], [Ptc])
                    K.tt("dve", Tt.ap(hh * n, hh * n + n, 0, n), G2.ap(hh * 2 * n, hh * 2 * n + n, 0, n), ident.ap(0, n, 0, n), ALU.add, [G2, ident], [Tt])
                for lev in range(1, nlev):
                    Pn, Ptn = Pb[lev % 2], Ptb[lev % 2]
                    ps1 = K.ps()
                    for hh in range(4):
                        K.mm(ps1.ap(hh * n, hh * n + n, 0, n), Ptc.ap(hh * n, hh * n + n, 0, n), Pc.ap(hh * n, hh * n + n, 0, n), True, True, [Pc, Ptc], [ps1])
                    K.cp("act", Pn.ap(0, 4 * n, 0, n), ps1.ap(0, 4 * n, 0, n), [ps1], [Pn])
                    if lev < nlev - 1:
                        ps2 = K.ps()
                        for hh in range(4):
                            K.mm(ps2.ap(hh * n, hh * n + n, 0, n), Pc.ap(hh * n, hh * n + n, 0, n), Ptc.ap(hh * n, hh * n + n, 0, n), True, True, [Pc, Ptc], [ps2])
                        K.cp("act", Ptn.ap(0, 4 * n, 0, n), ps2.ap(0, 4 * n, 0, n), [ps2], [Ptn])
                    ps3 = K.ps()
                    for hh in range(4):
                        K.mm(ps3.ap(hh * n, hh * n + n, 0, n), Pn.ap(hh * n, hh * n + n, 0, n), Tt.ap(hh * n, hh * n + n, 0, n), True, True, [Pn, Tt], [ps3])
                    K.tt("dve", Tt.ap(0, 4 * n, 0, n), Tt.ap(0, 4 * n, 0, n), ps3.ap(0, 4 * n, 0, n), ALU.add, [Tt, ps3], [Tt])
                    Pc, Ptc = Pn, Ptn

                tt_ap = lambda hh: Tt.ap(hh * n, hh * n + n, 0, n)
            else:
                for bi_, bdst in enumerate((PTb[0], QTb[0])):
                    for hh in range(4):
                        K.cp("pool" if (hh + bi_) % 2 else "dve", r32(bdst, hh * 256 + 128, hh * 256 + 256), ident.ap(0, 128), [ident], [bdst])
                K.cp("act", slot(PTb[0], 0, 4, 0), GN.ap(0, 512).rearrange("p (h c) -> p h c", c=128), [GN], [PTb[0]])
                K.cp("act", slot(QTb[0], 0, 4, 0), G2.ap(0, 1024).rearrange("p (h c) -> p h c", c=256)[:, :, 0:128], [G2], [QTb[0]])
                NLV = 7
                for lev in range(NLV):
                    cur, nx = lev % 2, (lev + 1) % 2
                    last = lev == NLV - 1
                    for pp in range(2):
                        if not last:
                            psA_ = K.ps()
                            for h2 in range(2):
                                hh = 2 * pp + h2
                                K.mm(psA_.ap(h2 * 256, h2 * 256 + 256), r32(QTb[cur], hh * 256, hh * 256 + 128), r32(PTb[cur], hh * 256, hh * 256 + 256), True, True, [QTb[cur], PTb[cur]], [psA_])
                            pv = psA_.ap(0, 512).rearrange("p (h c) -> p h c", c=256)
                            K.cp("act", slot(PTb[nx], 2 * pp, 2, 0), pv[:, :, 0:128], [psA_], [PTb[nx]])
                            K.tt("dve", slot(PTb[nx], 2 * pp, 2, 1), slot(PTb[cur], 2 * pp, 2, 1).bitcast(F32), pv[:, :, 128:256], ALU.add, [psA_, PTb[cur]], [PTb[nx]])
                        psB_ = K.ps()
                        for h2 in range(2):
                            hh = 2 * pp + h2
                            K.mm(psB_.ap(h2 * 256, h2 * 256 + 256), r32(PTb[cur], hh * 256, hh * 256 + 128), r32(QTb[cur], hh * 256, hh * 256 + 256), True, True, [QTb[cur], PTb[cur]], [psB_])
                        pv = psB_.ap(0, 512).rearrange("p (h c) -> p h c", c=256)
                        if not last:
                            K.cp("act", slot(QTb[nx], 2 * pp, 2, 0), pv[:, :, 0:128], [psB_], [QTb[nx]])
                        K.tt("dve", slot(QTb[nx], 2 * pp, 2, 1), slot(QTb[cur], 2 * pp, 2, 1).bitcast(F32), pv[:, :, 128:256], ALU.add, [psB_, QTb[cur]], [QTb[nx]])
                QF = QTb[NLV % 2]
                Tt = QF
                tt_ap = lambda hh, QF=QF: QF.ap(hh * 256 + 128, hh * 256 + 256).bitcast(F32)
            psXb = {0: K.ps(), 64: K.ps()}
            for b in (0, 64):
                psX = psXb[b]
                for jj in range(2):
                    j = 2 * hg + jj
                    for s in range(nseq):
                        K.mm(psX.ap(jj * n + s * L, jj * n + s * L + L, b, b + 64), Hs[s].ap(j * 64, j * 64 + 64, b, b + 64),
                             AR[j].ap(s * L, s * L + L, b, b + 64), True, True, [Hs[s], AR[j]], [psX])
                K.cp("act" if b == 0 else "dve", X0T.ap(0, 2 * n, b, b + 64), psX.ap(0, 2 * n, b, b + 64), [psX], [X0T])
            psXt = K.ps()
            for jj in range(2):
                K.tr(psXt.ap(jj * 128, jj * 128 + 128, 0, n), X0T.ap(jj * n, jj * n + n), ident.ap(0, 128), [X0T, ident], [psXt])
            K.cp("dve", X0.ap(0, 256, 0, n), psXt.ap(0, 256, 0, n), [psXt], [X0])
            psX2 = K.ps()
            for hh, h in enumerate(heads):
                K.mm(psX2.ap(hh * 64, hh * 64 + 64, 0, n), G3.ap(hh * 2 * n, hh * 2 * n + n, 0, n), Vtm.ap(h * 64, h * 64 + 64, 0, n), True, True, [G3, Vtm], [psX2])
            K.tt("dve", Xs.ap(0, 256, 0, n), psX2.ap(0, 256, 0, n), X0.ap(0, 256, 0, n), ALU.add, [psX2, X0], [Xs])
            psU = K.ps()
            for hh, h in enumerate(heads):
                K.mm(psU.ap(hh * 64, hh * 64 + 64, 0, n), tt_ap(hh), Xs.ap(hh * 64, hh * 64 + 64, 0, n), True, True, [Tt, Xs], [psU])
            K.cp("act", Ut.ap(hg * 256, hg * 256 + 256, 0, n), psU.ap(0, 256, 0, n), [psU], [Ut])
            psOb = {0: K.ps(), 64: K.ps()}
            for hh, h in enumerate(heads):
                j, b = h // 2, 64 * (h % 2)
                jj = hh // 2
                psO = psOb[b]
                oa = psO.ap(jj * n, jj * n + n, b, b + 64)
                K.mm(oa, Ut.ap(h * 64, h * 64 + 64, 0, n), G2.ap(hh * 2 * n + n, hh * 2 * n + 2 * n, 0, n), True, False, [Ut, G2], [psO])
                K.mm(oa, Vtm.ap(h * 64, h * 64 + 64, 0, n), G3.ap(hh * 2 * n + n, hh * 2 * n + 2 * n, 0, n), False, False, [Vtm, G3], [psO])
                for s in range(nseq):
                    K.mm(psO.ap(jj * n + s * L, jj * n + s * L + L, b, b + 64), Hs[s].ap(j * 64, j * 64 + 64, b, b + 64),
                         AR[j].ap(n + s * L, n + s * L + L, b, b + 64), False, s == nseq - 1, [Hs[s], AR[j]], [psO])
            for jj in range(2):
                for b in (0, 64):
                    K.cp("act" if b == 0 else "dve", OT[2 * hg + jj].ap(0, n, b, b + 64), psOb[b].ap(jj * n, jj * n + n, b, b + 64), [psOb[b]], [OT[2 * hg + jj]])
            for s in range(nseq):
                if nseq > 1:
                    UVm = UVms[0]
                    uvi[0] += 1
                    K.ts("dve", UVm.ap(0, 256, 0, n), Ut.ap(hg * 256, hg * 256 + 256, 0, n), C["rowmask"].ap(s, s + 1, 0, n), None, ALU.mult, None, [Ut, C["rowmask"]], [UVm])
                    K.act(UVm.ap(256, 512, 0, n), Vtm.ap(hg * 256, hg * 256 + 256, 0, n), AF.Copy, [Vtm, C["rowmask"]], [UVm], scale=C["rowmask"].ap(s, s + 1, 0, n))
                    ua = lambda hh: UVm.ap(hh * 64, hh * 64 + 64, 0, n)
                    va = lambda hh: UVm.ap(256 + hh * 64, 256 + hh * 64 + 64, 0, n)
                    ub = [UVm]
                else:
                    ua = lambda hh: Ut.ap((4 * hg + hh) * 64, (4 * hg + hh) * 64 + 64, 0, n)
                    va = lambda hh: Vtm.ap((4 * hg + hh) * 64, (4 * hg + hh) * 64 + 64, 0, n)
                    ub = [Ut, Vtm]
                psH = K.ps()
                for hh, h in enumerate(heads):
                    j, b = h // 2, 64 * (h % 2)
                    jj = hh // 2
                    oa = psH.ap(jj * 64, jj * 64 + 64, b, b + 64)
                    K.mm(oa, Btm.ap(h * 64, h * 64 + 64, 0, n), ua(hh), True, False, [Btm] + ub, [psH])
                    K.mm(oa, Ktm.ap(h * 64, h * 64 + 64, 0, n), va(hh), False, True, [Ktm] + ub, [psH])
                for jj in range(2):
                    j = 2 * hg + jj
                    hsub = Hs[s].sub(j * 64, 64)
                    K.ts("pool", hsub.ap(0, 64), hsub.ap(0, 64), PCc[j].ap(s, s + 1), None, ALU.mult, None, [hsub, PCc[j]], [hsub])
                    K.stt(hsub.ap(0, 64), psH.ap(jj * 64, jj * 64 + 64), PCc[j].ap(s, s + 1), hsub.ap(0, 64), ALU.mult, ALU.add, [psH, PCc[j], hsub], [hsub])
        if ch["last"]:
            dst = (dr["o_wkv_s"] if sample else dr["o_wkv_p"]).rearrange("(s h v) k -> s v h k", h=12, v=64)
            for s in range(nseq):
                st = sout()
                for g0 in (0, 4):
                    g = list(range(g0, min(6, g0 + 4)))
                    ps = K.ps()
                    for i, j in enumerate(g):
                        K.tr(ps.ap(i * 128, i * 128 + 128, 0, 64), Hs[s].ap(j * 64, j * 64 + 64), ident.ap(0, 128), [Hs[s], ident], [ps])
                    K.cp(K.ev_eng(), st.ap(g0 * 128, (g0 + len(g)) * 128, 0, 64), ps.ap(0, len(g) * 128, 0, 64), [ps], [st])
                K.store(st, dst[s], st.ap(0, 768, 0, 64).rearrange("p (h k) -> p h k", k=64))
        for j in range(6):
            a, b = t1[j], t2[j]
            ps = K.ps()
            K.mm(ps.ap(0, n), bd1.ap(0, 128), OT[j].ap(0, n), True, True, [bd1, OT[j]], [ps])
            K.stt(a.ap(0, n), ps.ap(0, n), -1.0 / 64, OT[j].ap(0, n), ALU.mult, ALU.add, [ps, OT[j]], [a])
            K.tt("pool", b.ap(0, n), a.ap(0, n), a.ap(0, n), ALU.mult, [a], [b])
            ps = K.ps()
            K.mm(ps.ap(0, n), bd1.ap(0, 128), b.ap(0, n), True, True, [bd1, b], [ps])
            K.act(b.ap(0, n), ps.ap(0, n), AF.Ln, [ps, C["pcols"]], [b], bias=col("c_epsgn"), scale=1.0 / 64)
            K.act(b.ap(0, n), b.ap(0, n), AF.Exp, [b], [b], scale=-0.5)
            K.stt(a.ap(0, n), a.ap(0, n), col("gn_g", j), b.ap(0, n), ALU.mult, ALU.mult, [a, b, C["pcols"]], [a])
            K.tt("dve", a.ap(0, n), a.ap(0, n), BON[j].ap(0, n), ALU.add, [a, BON[j]], [a])
            K.tt("dve", ym[2 + j].apb(o, o + n), a.ap(0, n), Gg[j].ap(0, n), ALU.mult, [a, Gg[j]], [ym[2 + j]])
        K.top = m0

    def gelu(dst_ap, src_ap, n, tmp, R, Wb):
        K.act(dst_ap, src_ap, AF.Gelu_apprx_tanh, R, Wb)

    def gmlp(ch, o):
        n, nseq, L = ch["n"], ch["nseq"], ch["L"]
        sample = ch["sample"]
        m0 = K.top
        Z = [K.alloc(n) for _ in range(8)]
        tmp = [K.alloc(n) for _ in range(8)]
        for q in range(8):
            gelu(Z[q].ap(0, n), pr[q].ap(o, o + n), n, tmp[q], [pr[q]], [Z[q]])
        Zu, Zv = Z[0:4], Z[4:8]
        ps = K.ps()
        for k in range(4):
            K.mm(ps.ap(0, n), ones.ap(0, 128), Zv[k].ap(0, n), k == 0, k == 3, [ones, Zv[k]], [ps])
        ps2 = K.ps()
        for k in range(4):
            K.stt(Zv[k].ap(0, n), ps.ap(0, n), -1.0 / 512, Zv[k].ap(0, n), ALU.mult, ALU.add, [ps, Zv[k]], [Zv[k]])
            t = tmp[k]
            K.tt("pool", t.ap(0, n), Zv[k].ap(0, n), Zv[k].ap(0, n), ALU.mult, [Zv[k]], [t])
            K.mm(ps2.ap(0, n), ones.ap(0, 128), t.ap(0, n), k == 0, k == 3, [ones, t], [ps2])
        rs = K.alloc(n)
        K.act(rs.ap(0, n), ps2.ap(0, n), AF.Ln, [ps2, C["pcols"]], [rs], bias=col("c_epsln"), scale=1.0 / 512)
        K.act(rs.ap(0, n), rs.ap(0, n), AF.Exp, [rs], [rs], scale=-0.5)
        for k in range(4):
            K.tt("dve", Zv[k].ap(0, n), Zv[k].ap(0, n), rs.ap(0, n), ALU.mult, [Zv[k], rs], [Zv[k]])
            K.ts("dve", Zv[k].ap(0, n), Zv[k].ap(0, n), col("ln_g", k), col("ln_b", k), ALU.mult, ALU.add, [Zv[k], C["pcols"]], [Zv[k]])
        if sample:
            fm2tm_store([(Zv[k].ap(0, n), [Zv[k]]) for k in range(4)], n, dr["o_gv_s"][:, :])
            acc = K.alloc(64)
            for h in range(4):
                vv = Zv[h].ap(0, n).rearrange("p (s l) -> p s l", l=L)
                av = acc.ap(0, n).rearrange("p (s l) -> p s l", l=L)
                for t in range(L):
                    wc = lambda tp: C["wsb"].ap(h * 16 + t * 4 + tp, h * 16 + t * 4 + tp + 1)
                    K.ts("dve", av[:, :, t], vv[:, :, 0], wc(0), C["bs_bc"].ap(h * 128 + t, h * 128 + t + 1), ALU.mult, ALU.add, [Zv[h], C["wsb"], C["bs_bc"]], [acc])
                    for tp in range(1, t + 1):
                        K.stt(av[:, :, t], vv[:, :, tp], wc(tp), av[:, :, t], ALU.mult, ALU.add, [Zv[h], acc, C["wsb"]], [acc])
                K.tt("dve", ym[h].apb(o, o + n), Zu[h].ap(0, n), acc.ap(0, n), ALU.mult, [Zu[h], acc], [ym[h]])
        else:
            for c in range(n // 128):
                c0_ = c * 128
                Vt = K.alloc(512)
                psT = K.ps()
                for k in range(4):
                    K.tr(psT.ap(k * 128, k * 128 + 128), Zv[k].ap(c0_, c0_ + 128), ident.ap(0, 128), [Zv[k], ident], [psT])
                K.cp("act", Vt.ap(0, 512), psT.ap(0, 512), [psT], [Vt])
                for h in range(4):
                    ps = K.ps()
                    K.mm(ps.ap(0, 128), Vt.ap(h * 128, h * 128 + 128), C["wmT"].ap(h * 128, h * 128 + 128), True, True, [Vt, C["wmT"]], [ps])
                    t = tmp[4 + h].sub(c0_, 128)
                    K.tt("dve", t.ap(0, 128), ps.ap(0, 128), C["bs_bc"].ap(h * 128, h * 128 + 128), ALU.add, [ps, C["bs_bc"]], [t])
                    K.tt("pool", ym[h].apb(o + c0_, o + c0_ + 128), Zu[h].ap(c0_, c0_ + 128), t.ap(0, 128), ALU.mult, [Zu[h], t], [ym[h]])
        K.top = m0

    def rglru(ch, o):
        n, nseq, L = ch["n"], ch["nseq"], ch["L"]
        sample = ch["sample"]
        Wd = 3 + L
        m0 = K.top
        rm = C["rm_s"] if sample else C["rm_p"]
        XC = [K.alloc(n) for _ in range(4)]
        GX = [K.alloc(n) for _ in range(4)]
        GA = [K.alloc(n) for _ in range(4)]
        HS = [K.alloc(n) for _ in range(4)]
        tmp = [K.alloc(n) for _ in range(4)]
        gg = [K.alloc(64) for _ in range(4)] + [K.alloc(n) for _ in range(4)]
        for k in range(4):
            e = EC[k]
            if not ch["first"]:
                K.cp("pool", v3(e, nseq, Wd, 0, 3), v3(e, nseq, Wd, L, 3), [e], [e])
            K.cp("act", v3(e, nseq, Wd, 3, L), pr[12 + k].ap(o, o + n).rearrange("p (s l) -> p s l", l=L), [pr[12 + k]], [e])
            xv = XC[k].ap(0, n).rearrange("p (s l) -> p s l", l=L)
            K.ts("dve", xv, v3(e, nseq, Wd, 0, L), col("conv_w0", k), col("conv_b", k), ALU.mult, ALU.add, [e, C["pcols"]], [XC[k]])
            for jt in range(1, 4):
                K.stt(xv, v3(e, nseq, Wd, jt, L), col("conv_w%d" % jt, k), xv, ALU.mult, ALU.add, [e, XC[k], C["pcols"]], [XC[k]])
            ps = K.ps()
            K.mm(ps.ap(0, n), C["wx_bd"].ap(k * 128, k * 128 + 128), XC[k].ap(0, n), True, True, [C["wx_bd"], XC[k]], [ps])
            K.act(GX[k].ap(0, n), ps.ap(0, n), AF.Sigmoid, [ps, C["pcols"]], [GX[k]], bias=col("bx", k))
            ps = K.ps()
            K.mm(ps.ap(0, n), C["wa_bd"].ap(k * 128, k * 128 + 128), XC[k].ap(0, n), True, True, [C["wa_bd"], XC[k]], [ps])
            K.act(GA[k].ap(0, n), ps.ap(0, n), AF.Sigmoid, [ps, C["pcols"]], [GA[k]], bias=col("ba", k))
            K.act(GA[k].ap(0, n), GA[k].ap(0, n), AF.Exp, [GA[k], nsp8], [GA[k]], scale=nsp8.ap(k, k + 1))
            t = tmp[k]
            K.tt("pool", HS[k].ap(0, n), GX[k].ap(0, n), XC[k].ap(0, n), ALU.mult, [GX[k], XC[k]], [HS[k]])
            K.tt("pool", t.ap(0, n), GA[k].ap(0, n), GA[k].ap(0, n), ALU.mult, [GA[k]], [t])
            K.ts("dve", t.ap(0, n), t.ap(0, n), 0.9999999, None, ALU.min, None, [t], [t])
            K.act(t.ap(0, n), t.ap(0, n), AF.Ln, [t, C["pcols"]], [t], bias=col("c_one"), scale=-1.0)
            K.act(t.ap(0, n), t.ap(0, n), AF.Exp, [t], [t], scale=0.5)
            K.tt("dve", t.ap(0, n), t.ap(0, n), HS[k].ap(0, n), ALU.mult, [t, HS[k]], [t])
            tv = t.ap(0, n).rearrange("p (s l) -> p s l", l=L)
            av = GA[k].ap(0, n).rearrange("p (s l) -> p s l", l=L)
            h0 = gg[k]
            K.tt("dve", h0.ap(0, nseq), av[:, :, 0], HL[k].ap(0, nseq), ALU.mult, [GA[k], HL[k]], [h0])
            K.tt("dve", tv[:, :, 0], tv[:, :, 0], h0.ap(0, nseq), ALU.add, [t, h0], [t])
            K.memset("pool", av[:, :, 0], 0.0, [GA[k]])
            K.scan(HS[k].ap(0, n), GA[k].ap(0, n), t.ap(0, n), 0.0, [GA[k], t], [HS[k]])
            K.cp("pool", HL[k].ap(0, nseq), HS[k].ap(0, n).rearrange("p (s l) -> p s l", l=L)[:, :, L - 1], [HS[k]], [HL[k]])
            g = gg[4 + k]
            gelu(g.ap(0, n), pr[8 + k].ap(o, o + n), n, None, [pr[8 + k]], [g])
            K.tt("dve", ym[4 + k].apb(o, o + n), HS[k].ap(0, n), g.ap(0, n), ALU.mult, [HS[k], g], [ym[4 + k]])
        if ch["last"]:
            stc = [K.alloc(64) for _ in range(4)]
            for k in range(4):
                K.cp("pool", stc[k].ap(0, nseq * 3).rearrange("p (s l) -> p s l", l=3), v3(EC[k], nseq, Wd, L, 3), [EC[k]], [stc[k]])
            fm2tm_store([(stc[k].ap(0, nseq * 3), [stc[k]]) for k in range(4)], nseq * 3, (dr["o_conv_s"] if sample else dr["o_conv_p"])[:, :])
            fm2tm_store([(HL[k].ap(0, nseq), [HL[k]]) for k in range(4)], nseq, (dr["o_lru_s"] if sample else dr["o_lru_p"])[:, :])
        K.top = m0

    blocks = []
    for b in range(SEQ // BN):
        chs = []
        for c in range(BLKC):
            gi = b * BLKC + c
            chs.append(dict(o=c * 128, n=128, nseq=1, L=128, first=(gi == 0), last=(gi == SEQ // 128 - 1), sample=False))
        blocks.append(dict(c0=b * BN, n=BN, chunks=chs, sample=False))
    blocks.append(dict(c0=SEQ, n=NS, chunks=[dict(o=0, n=NS, nseq=SB, L=SL, first=True, last=True, sample=True)], sample=True))

    def v3e(buf, nseq, width, c0, ln):
        return K.arena[:, buf.off:buf.off + nseq * width].rearrange("p (s w) -> p s w", w=width)[:, :, c0:c0 + ln]

    scr = [RegObj() for _ in range(8)]
    for blk in blocks:
        c0, n = blk["c0"], blk["n"]
        for r0 in range(0, n, 128):
            m = min(128, n - r0)
            tm2fm_load(dr["xin"][c0 + r0:c0 + r0 + m, :], m, 8,
                       [(xb[k].ap(r0, r0 + m), [xb[k]], None) for k in range(8)])
        if blk["sample"]:
            for j in range(2):
                K.memset("pool", EP[j].ap(), 0.0, [EP[j]])
            for r in range(2):
                tm2fm_load(dr["st_pool"][r * 120:(r + 1) * 120, :], 120, 2,
                           [(K.arena[:, EP[j].off:EP[j].off + 304].rearrange("p (s w) -> p s w", w=19)[:, r * 8:(r + 1) * 8, 0:15], [EP[j]], 15) for j in range(2)])
            for g0 in range(0, 20, 8):
                g = list(range(g0, min(20, g0 + 8)))
                tm2fm_load(dr["st_shift"][:, g0 * 128:(g0 + len(g)) * 128], SB, len(g),
                           [(K.arena[:, ES.off + q * 129:ES.off + q * 129 + SB * 5].rearrange("p (s w) -> p s w", w=5)[:, :, 0], [ES], None) for q in g])
        rmsnorm(xb, "ev_norm_g", hb, n)

        def sink0(m, ps, n=n):
            K.cp(K.ev_eng(), pr[m].ap(0, n), ps.ap(0, n), [ps], [pr[m]])

        linear("ev_w_in", None, [(hb[k].apb(0, n), [hb[k]]) for k in range(8)], 2816, n, sink0)
        for ch in blk["chunks"]:
            pool_mixer(ch, ch["o"])
            rwkv(ch, ch["o"])

        def sinkx0(m, ps, n=n):
            K.tt("dve", xb[m].ap(0, n), xb[m].ap(0, n), ps.ap(0, n), ALU.add, [xb[m], ps], [xb[m]])

        linear("ev_w_out", None, [(ym[k].apb(0, n), [ym[k]]) for k in range(8)], D, n, sinkx0)
        for k in range(8):
            K.store(xb[k], dr["xscr"][k, :, c0:c0 + n], xb[k].ap(0, n), W=[scr[k]])
    K.top = base_top
    xall = [K.alloc(NTOK) for _ in range(8)]
    for k in range(8):
        K.load(xall[k], xall[k].ap(0, NTOK), dr["xscr"][k], R=[scr[k]])
    ffn_all(0, xall)
    for blk in blocks:
        c0, n = blk["c0"], blk["n"]
        xv = [xall[k].sub(c0, n) for k in range(8)]
        if blk["sample"]:
            tm2fm_load(dr["st_conv"][:, :], SB * 3, 4,
                       [(K.arena[:, EC[k].off:EC[k].off + SB * 7].rearrange("p (s w) -> p s w", w=7)[:, :, 0:3], [EC[k]], 3) for k in range(4)])
            tm2fm_load(dr["st_lru"][:, :], SB, 4, [(HL[k].ap(0, SB), [HL[k]], None) for k in range(4)])
        rmsnorm(xv, "od_norm_g", hb, n)

        def sink1(m, ps, n=n):
            K.cp(K.ev_eng(), pr[m].ap(0, n), ps.ap(0, n), [ps], [pr[m]])

        linear("od_w_in", None, [(hb[k].apb(0, n), [hb[k]]) for k in range(8)], 2048, n, sink1)
        if blk["sample"]:
            chs1 = blk["chunks"]
        else:
            chs1 = [dict(o=0, n=n, nseq=1, L=n, first=blk["chunks"][0]["first"], last=blk["chunks"][-1]["last"], sample=False)]
        for ch in chs1:
            gmlp(ch, ch["o"])
            rglru(ch, ch["o"])

        def sinkx1(m, ps, n=n, xv=xv):
            K.tt("dve", xv[m].ap(0, n), xv[m].ap(0, n), ps.ap(0, n), ALU.add, [xv[m], ps], [xv[m]])

        linear("od_w_out", None, [(ym[k].apb(0, n), [ym[k]]) for k in range(8)], D, n, sinkx1)
    ffn_all(1, xall)
    mfin = K.top
    yn = [K.alloc(512) for _ in range(8)]
    for (t0, tn) in TBS:
        rmsnorm([xall[k].sub(t0, tn) for k in range(8)], "fin_g", yn, tn, bf=False)
        for r0 in range(0, tn, 128):
            m = min(128, tn - r0)
            fm2tm_store([(yn[k].ap(r0, r0 + m), [yn[k]]) for k in range(8)], m, dr["y"][t0 + r0:t0 + r0 + m, :])
    K.top = mfin


def _cols(v):
    v = np.asarray(v, np.float32).reshape(-1, 128)
    return np.ascontiguousarray(v.T)


def _bd(blocks):
    nt = len(blocks) // 2
    out = np.zeros((128, nt * 128), np.float32)
    for i, b in enumerate(blocks):
        t, h = i // 2, i % 2
        out[h * 64:(h + 1) * 64, t * 128 + h * 64:t * 128 + (h + 1) * 64] = b
    return out


def _host_consts(inp):
    c = {}
    pc = np.zeros((128, CONST_COLS["pcols"]), np.float32)

    def put(nm, arr):
        o, n = PC[nm]
        pc[:, o:o + n] = _cols(arr)

    put("ev_norm_g", inp["ev_norm_g"][0]); put("pool_scale", inp["pool_scale"][0]); put("mu", inp["rwkv_mu"][0])
    put("w0", inp["rwkv_w0"][0]); put("a0", inp["rwkv_a0"][0]); put("k_k", inp["rwkv_k_k"][0]); put("k_a", inp["rwkv_k_a"][0])
    put("r_k", inp["rwkv_r_k"][0].reshape(-1)); put("gn_g", inp["rwkv_gn_g"][0]); put("gn_b", inp["rwkv_gn_b"][0])
    put("od_norm_g", inp["od_norm_g"][0]); put("ln_g", inp["gmlp_ln_g"][0]); put("ln_b", inp["gmlp_ln_b"][0])
    for j in range(4):
        put("conv_w%d" % j, inp["lru_conv_w"][0, j])
    put("conv_b", inp["lru_conv_b"][0]); put("bx", inp["lru_bx"][0]); put("ba", inp["lru_ba"][0]); put("lam", inp["lru_lam"][0])
    put("ff_g0", inp["ff_norm_g"][0]); put("ff_g1", inp["ff_norm_g"][1]); put("fin_g", inp["final_norm_g"])
    pc[:, PC["c_eps6"][0]] = 1e-6
    pc[:, PC["c_epsgn"][0]] = 64e-5
    pc[:, PC["c_epsln"][0]] = 1e-5
    pc[:, PC["c_one"][0]] = 1.0
    pc[:, PC["c_tiny"][0]] = 1e-24
    c["pcols"] = pc
    c["ident"] = np.eye(128, dtype=np.float32)
    bd = np.zeros((128, 128), np.float32)
    bd[:64, :64] = 1.0
    bd[64:, 64:] = 1.0
    c["bd1"] = bd
    c["ones"] = np.ones((128, 128), np.float32)
    i = np.arange(128)
    SLm = (i[:, None] > i[None, :]).astype(np.float32)
    SU = (i[:, None] < i[None, :]).astype(np.float32)
    IU = (i[:, None] <= i[None, :]).astype(np.float32)
    c["mN_p"] = SLm
    c["m2_p"] = np.concatenate([SU, IU], 1)
    i = np.arange(64)
    same = (i[:, None] // 4 == i[None, :] // 4)
    mNs = np.zeros((128, 64), np.float32)
    mNs[:64] = (same & (i[:, None] > i[None, :])).astype(np.float32)
    c["mN_s"] = mNs
    m2s = np.zeros((128, 128), np.float32)
    m2s[:64, :64] = (same & (i[:, None] < i[None, :])).astype(np.float32)
    m2s[:64, 64:] = (same & (i[:, None] <= i[None, :])).astype(np.float32)
    c["m2_s"] = m2s
    rmp = np.ones((128, 128), np.float32)
    rmp[:, 0] = 0.0
    c["rm_p"] = rmp
    rms = np.ones((128, 64), np.float32)
    rms[:, 0::4] = 0.0
    c["rm_s"] = rms
    rmk = np.zeros((128, 16), np.float32)
    rmk[:64] = (i[:, None] // 4 == np.arange(16)[None, :]).astype(np.float32)
    c["rowmask"] = rmk
    wins = np.array([2, 4, 8, 16], np.float32)
    inv0 = np.zeros((128, 2, 128), np.float32)
    inv1 = np.zeros((128, 2, 128), np.float32)
    pos = np.arange(128, dtype=np.float32)
    for t in range(2):
        for hlf in range(2):
            w = wins[2 * t + hlf]
            inv0[hlf * 64:(hlf + 1) * 64, t, :] = 1.0 / np.minimum(w, pos + 1)[None, :]
            inv1[hlf * 64:(hlf + 1) * 64, t, :] = 1.0 / w
    c["invc0"] = inv0.reshape(128, 256)
    c["invc1"] = inv1.reshape(128, 256)
    c["wa_w2"] = np.concatenate([inp["rwkv_w_w2"][0], inp["rwkv_a_w2"][0]], 0).astype(np.float32)
    c["g_w2"] = np.ascontiguousarray(inp["rwkv_g_w2"][0])
    c["poolw_bd"] = _bd([inp["pool_w"][0, g] for g in range(4)])
    c["wx_bd"] = _bd([inp["lru_wx"][0, g] for g in range(8)])
    c["wa_bd"] = _bd([inp["lru_wa"][0, g] for g in range(8)])
    ws = inp["gmlp_ws"][0]
    c["wmT"] = np.ascontiguousarray(np.transpose(ws, (2, 0, 1)).reshape(128, 512))
    c["bs_bc"] = np.ascontiguousarray(np.broadcast_to(inp["gmlp_bs"][0].reshape(1, 512), (128, 512)))
    c["wsb"] = np.ascontiguousarray(np.broadcast_to(ws[:, :4, :4].reshape(1, 64), (128, 64)))

    return {k: np.ascontiguousarray(v, dtype=np.float32) for k, v in c.items()}


_NC_CACHE = {}


def kernel(**inp):
    inp = {k: np.asarray(v) for k, v in inp.items()}
    if "nc" not in _NC_CACHE:
        _NC_CACHE["nc"] = build_program()
    nc = _NC_CACHE["nc"]
    consts = _host_consts(inp)
    shared = dict(consts)
    shared["ev_w_in"] = np.ascontiguousarray(inp["ev_w_in"][0])
    shared["ev_w_out"] = np.ascontiguousarray(inp["ev_w_out"][0])
    shared["od_w_in"] = np.ascontiguousarray(inp["od_w_in"][0])
    shared["od_w_out"] = np.ascontiguousarray(inp["od_w_out"][0])
    shared["ff_w1"] = np.ascontiguousarray(inp["ff_w1"])
    shared["ff_w2"] = np.ascontiguousarray(inp["ff_w2"])
    in_maps = []
    for c in range(NCORE):
        s0, s1 = c * SB, (c + 1) * SB
        m = dict(shared)
        m["xin"] = np.ascontiguousarray(np.concatenate([inp["x_prompt"][c], inp["x_sample"][s0:s1].reshape(NS, D)], 0))
        m["st_pool"] = np.ascontiguousarray(inp["state_pool"][0, s0:s1].reshape(SB * 15, 256))
        m["st_shift"] = np.ascontiguousarray(inp["state_shift"][0, s0:s1])
        m["st_wkv"] = np.ascontiguousarray(inp["state_wkv"][0, s0:s1].reshape(SB * 768, 64))
        m["st_conv"] = np.ascontiguousarray(inp["state_conv"][0, s0:s1].reshape(SB * 3, 512))
        m["st_lru"] = np.ascontiguousarray(inp["state_lru"][0, s0:s1])
        in_maps.append(m)
    res = run_bass_kernel_spmd(nc, in_maps, core_ids=list(range(NCORE)))
    R = res.results
    cat = lambda nm: np.stack([np.asarray(R[c][nm]) for c in range(NCORE)], 0)
    y = cat("y")
    y_prompt = np.ascontiguousarray(y[:, :SEQ, :])
    y_sample = np.ascontiguousarray(y[:, SEQ:, :].reshape(NCORE * SB, SL, D))
    p_pool = cat("o_pool_p")[None]
    p_shift = cat("o_shift_p").reshape(1, NCORE, 2560)
    p_wkv = cat("o_wkv_p").reshape(1, NCORE, 12, 64, 64)
    p_conv = cat("o_conv_p")[None]
    p_lru = cat("o_lru_p").reshape(1, NCORE, 512)
    s_pool = cat("o_pool_s").reshape(1, NCORE * SB, 15, 256)
    s_shift = cat("o_shift_s").reshape(1, NCORE * SB, 2560)
    s_wkv = cat("o_wkv_s").reshape(1, NCORE * SB, 12, 64, 64)
    s_conv = cat("o_conv_s").reshape(1, NCORE * SB, 3, 512)
    s_lru = cat("o_lru_s").reshape(1, NCORE * SB, 512)
    s_gv = cat("o_gv_s").reshape(1, NCORE * SB, SL, 512)
    outs = (y_prompt, y_sample, p_pool, p_shift, p_wkv, p_conv, p_lru, s_pool, s_shift, s_wkv, s_conv, s_lru, s_gv)
    return tuple(np.ascontiguousarray(o, dtype=np.float32) for o in outs)
```

```python
import math
import numpy as np
import concourse.bass as bass
import concourse.mybir as mybir
from concourse.bass_utils import run_bass_kernel_spmd

F32 = mybir.dt.float32
BF16 = mybir.dt.bfloat16
F32R = mybir.dt.float32r
AF = mybir.ActivationFunctionType
ALU = mybir.AluOpType

NCORE = 8
D = 1024
SEQ = 2048
SB = 16
SL = 4
NS = SB * SL
NTOK = SEQ + NS
BLKC = 2
BN = 128 * BLKC
ACT_TAB = {AF.Exp: "e", AF.Ln: "e", AF.Sigmoid: "s", AF.Tanh: "s", AF.Gelu_apprx_tanh: "g", AF.Sqrt: "q"}
GRAN = 64
FILL_NS, FILL_FRAC, FILL_MIN, FILL_MAX = 156.0, 0.5, 400.0, 20
SCHED = True
FILLERS = True


class Region:
    __slots__ = ("w", "rs")

    def __init__(self):
        self.w = None
        self.rs = []


class Chan:
    def __init__(self, sem):
        self.sem = sem
        self.count = 0


class Op:
    __slots__ = ("eng", "fn", "deps", "signal", "ordinal", "chan", "chan_val", "waits", "ndma", "name", "tiled",
                 "deps_all", "cost", "idx", "succ", "nwait", "ready_t", "dma_t", "rows", "tab")


class Prog:
    ENGS = ("pe", "dve", "act", "pool", "sp")

    def __init__(self, nc):
        self.nc = nc
        self.ops = {e: [] for e in self.ENGS}
        self.sems = {}
        self._ctx = []
        for e in self.ENGS:
            self.sems[e] = self._sem("s_" + e)
        self.nchan = 0
        self.order = []
        self.filler = None
        self.nfill = 0

    def make_filler(self, fn, dep):
        o = Op()
        o.eng = "pe"
        o.fn = fn
        o.deps = [dep]
        o.deps_all = [dep]
        o.signal = False
        o.ordinal = 0
        o.chan = None
        o.ndma = 0
        o.name = "fill"
        o.waits = None
        o.tiled = False
        o.rows = None
        o.tab = None
        o.cost = FILL_NS
        o.dma_t = 0.0
        return o

    def schedule(self, window=512, lat_x=900.0, lat_s=60.0):
        order = self.order
        for o in order:
            o.succ = []
            o.nwait = len(o.deps_all)
            o.ready_t = 0.0
        for o in order:
            for d in o.deps_all:
                d.succ.append(o)
        eng_free = {e: 0.0 for e in self.ENGS}
        new_ops = {e: [] for e in self.ENGS}
        last_tab = [None]
        TABLOAD = 1300.0
        win = []
        nxt = 0
        n = len(order)
        done = 0
        while done < n:
            while len(win) < window and nxt < n:
                win.append(order[nxt])
                nxt += 1
            best = None
            best_st = 0.0
            bi = -1
            for i, o in enumerate(win):
                if o.nwait:
                    continue
                st = eng_free[o.eng]
                if o.ready_t > st:
                    st = o.ready_t
                if o.tab is not None and o.tab != last_tab[0]:
                    st += TABLOAD
                if best is None or st < best_st:
                    best, best_st, bi = o, st, i
            o = best
            win.pop(bi)
            done += 1
            if o.eng == "pe" and self.filler is not None:
                gap = best_st - eng_free["pe"]
                if gap > FILL_MIN:
                    nf = min(FILL_MAX, int(gap * FILL_FRAC / FILL_NS))
                    for _ in range(nf):
                        new_ops["pe"].append(self.filler())
                    self.nfill += nf
            new_ops[o.eng].append(o)
            if o.tab is not None:
                last_tab[0] = o.tab
            eng_free[o.eng] = best_st + o.cost
            fin = best_st + o.cost + o.dma_t
            for sx in o.succ:
                lat = lat_s if (sx.eng == o.eng and o.chan is None) else lat_x
                t = fin + lat
                if t > sx.ready_t:
                    sx.ready_t = t
                sx.nwait -= 1
        self.ops = new_ops
        self.est_ns = max(eng_free.values())

    def _sem(self, name):
        cm = self.nc.semaphore(name)
        s = cm.__enter__()
        self._ctx.append(cm)
        return s

    def chan(self):
        self.nchan += 1
        return Chan(self._sem("ch%d" % self.nchan))

    def sbuf(self, name, shape, dtype):
        cm = self.nc.sbuf_tensor(name, shape, dtype)
        t = cm.__enter__()
        self._ctx.append(cm)
        return t

    def psum(self, name, shape, dtype):
        cm = self.nc.psum_tensor(name, shape, dtype)
        t = cm.__enter__()
        self._ctx.append(cm)
        return t

    limit = None
    nrec = 0

    def op(self, eng, fn, reads=(), writes=(), chan=None, ndma=0, name="", tiled=False, cost=200.0, dma_t=0.0, rows=None):
        self.nrec += 1
        if self.limit is not None and self.nrec > self.limit:
            return None
        o = Op()
        o.eng = eng
        o.fn = fn
        o.signal = False
        o.ordinal = 0
        o.chan = chan
        o.ndma = ndma
        o.name = name
        o.waits = None
        o.tiled = tiled
        o.rows = rows
        o.tab = None
        deps = []
        for r in reads:
            if r.w is not None:
                deps.append(r.w)
        for r in writes:
            if r.w is not None:
                deps.append(r.w)
            deps.extend(r.rs)
        seen = set()
        dd = []
        da = []
        for d in deps:
            if id(d) in seen or d is o:
                continue
            seen.add(id(d))
            da.append(d)
            if eng == "pe" and d.eng == "pe" and d.chan is None:
                ra, rb = rows, d.rows
                if ra is None or rb is None or not (ra[0] + ra[1] <= rb[0] or rb[0] + rb[1] <= ra[0]):
                    continue
            dd.append(d)
        o.deps = dd
        o.deps_all = da
        o.cost = cost
        o.dma_t = dma_t
        o.idx = len(self.order)
        self.order.append(o)
        if chan is not None:
            chan.count += ndma
            o.chan_val = 16 * chan.count
        for r in reads:
            r.rs.append(o)
        for r in writes:
            r.w = o
            r.rs = []
        self.ops[eng].append(o)
        return o

    def emit(self, final_wait_ops=()):
        nc = self.nc
        if final_wait_ops:
            o = Op()
            o.eng = "sp"
            o.fn = None
            o.deps = list(final_wait_ops)
            o.signal = False
            o.chan = None
            o.ndma = 0
            o.name = "final"
            o.deps_all = list(final_wait_ops)
            o.tiled = False
            o.rows = None
            o.tab = None
            o.ordinal = 0
            o.waits = None
            self.ops["sp"].append(o)
        for e in self.ENGS:
            for o in self.ops[e]:
                for d in o.deps:
                    if d.chan is None:
                        d.signal = True
        for e in self.ENGS:
            k = 0
            for o in self.ops[e]:
                if o.chan is None and o.signal:
                    k += 1
                    o.ordinal = k
        for e in self.ENGS:
            seen = {}
            for o in self.ops[e]:
                need = {}
                for d in o.deps:
                    if d.chan is not None:
                        key, val, sem = id(d.chan), d.chan_val, d.chan.sem
                    else:
                        key, val, sem = d.eng, d.ordinal, self.sems[d.eng]
                    if seen.get(key, 0) >= val:
                        continue
                    if key not in need or need[key][1] < val:
                        need[key] = (sem, val)
                for key, (sem, val) in need.items():
                    seen[key] = val
                o.waits = list(need.values())
        engmap = {"pe": "tensor", "dve": "vector", "act": "scalar", "pool": "gpsimd", "sp": "sync"}
        with nc.Block() as block:
            for e in self.ENGS:
                ops = self.ops[e]
                if not ops:
                    continue
                sem_e = self.sems[e]

                def body(eng, ops=ops, sem_e=sem_e):
                    for o in ops:
                        for sem, val in o.waits:
                            eng.wait_ge(sem, val)
                        if o.fn is None:
                            continue
                        r = o.fn(eng)
                        if o.chan is not None:
                            if not isinstance(r, (list, tuple)):
                                r = [r]
                            assert len(r) == o.ndma, (o.name, len(r), o.ndma)
                            for ins in r:
                                ins.then_inc(o.chan.sem, 16)
                        elif o.signal:
                            if isinstance(r, (list, tuple)):
                                r = r[-1]
                            assert r is not None, o.name
                            r.then_inc(sem_e, 1)

                getattr(block, engmap[e])(body)

    def close(self):
        for cm in reversed(self._ctx):
            cm.__exit__(None, None, None)
        self._ctx = []


class Buf:
    def __init__(self, K, off, n, ten=None, gr=None):
        self.K = K
        self.off = off
        self.n = n
        self._chan = None
        self.ten = ten if ten is not None else K.arena
        self.gr = gr if gr is not None else K.gran

    def ap(self, c0=0, c1=None, p0=0, p1=128):
        if c1 is None:
            c1 = self.n
        return self.ten[p0:p1, self.off + c0:self.off + c1]

    def apb(self, c0=0, c1=None, p0=0, p1=128):
        if c1 is None:
            c1 = 2 * self.n
        return self.K.arena[p0:p1, self.off:self.off + self.n].bitcast(BF16)[:, c0:c1]

    def v3(self, a, b, p0=0, p1=128):
        return self.K.arena[p0:p1, self.off:self.off + a * b].rearrange("p (a b) -> p a b", b=b)

    def sub(self, c0, n):
        return Buf(self.K, self.off + c0, n)

    def regs(self):
        g0 = self.off // GRAN
        g1 = (self.off + self.n - 1) // GRAN
        return self.gr[g0:g1 + 1]

    @property
    def chan(self):
        if self._chan is None:
            self._chan = self.K.P.chan()
        return self._chan


class PS:
    def __init__(self, t, ncols):
        self.t = t
        self.n = ncols
        self.r = [Region()]

    def ap(self, c0=0, c1=None, p0=0, p1=128):
        if c1 is None:
            c1 = self.n
        return self.t[p0:p1, c0:c1]

    def regs(self):
        return self.r


class RegObj:
    def __init__(self):
        self.r = [Region()]

    def regs(self):
        return self.r


def _regs(lst):
    out = []
    for b in lst:
        out.extend(b.regs())
    return out


class KB:
    def __init__(self, nc, dr):
        self.nc = nc
        self.dr = dr
        self.P = Prog(nc)
        self.NA = 53184 - 4096 + 2048
        self.arena = self.P.sbuf("arena", [128, self.NA], F32)
        self.arena2 = self.P.sbuf("arena_r", [128, 4096], F32R)
        self.gran2 = [Region() for _ in range(4096 // GRAN + 1)]
        self.gran = [Region() for _ in range(self.NA // GRAN + 2)]
        self.top = 0
        self.psb = [PS(self.P.psum("psb%d" % i, [128, 512], F32), 512) for i in range(8)]
        self.pi = 0
        self.stores = []
        self.rr = {"ev": 0, "el": 0}

    def alloc(self, n):
        n = (n + GRAN - 1) // GRAN * GRAN
        b = Buf(self, self.top, n)
        self.top += n
        assert self.top <= self.NA, self.top
        return b

    def ps(self):
        p = self.psb[self.pi % 7]
        self.pi += 1
        return p

    def ev_eng(self):
        self.rr["ev"] += 1
        return "act" if self.rr["ev"] % 2 else "dve"

    def el_eng(self):
        self.rr["el"] += 1
        return "pool" if self.rr["el"] % 2 else "dve"

    def _op(self, eng, fn, R, W, name="", tiled=False, cost=200.0, rows=None):
        Rr = [b for b in R if not isinstance(b, PS)]
        Ww = list(W) + [b for b in R if isinstance(b, PS) and b not in W]
        return self.P.op(eng, fn, _regs(Rr), _regs(Ww), name=name, tiled=tiled, cost=cost, rows=rows)

    @staticmethod
    def _ec(eng, out):
        fs = out.free_size()
        if eng == "pool":
            return 250.0 + fs * 1.6
        if eng == "act":
            return 220.0 + fs * 0.75
        return 120.0 + fs * 1.05

    def tt(self, eng, out, in0, in1, op, R, W):
        return self._op(eng, lambda e: e.tensor_tensor(out=out, in0=in0, in1=in1, op=op), R, W, "tt", cost=self._ec(eng, out))

    def ts(self, eng, out, in0, s1, s2, op0, op1, R, W):
        if op1 is None:
            return self._op(eng, lambda e: e.tensor_scalar(out=out, in0=in0, scalar1=s1, scalar2=None, op0=op0), R, W, "ts", cost=self._ec(eng, out))
        return self._op(eng, lambda e: e.tensor_scalar(out=out, in0=in0, scalar1=s1, scalar2=s2, op0=op0, op1=op1), R, W, "ts", cost=self._ec(eng, out))

    def stt(self, out, in0, sc, in1, op0, op1, R, W):
        return self._op("dve", lambda e: e.scalar_tensor_tensor(out=out, in0=in0, scalar=sc, in1=in1, op0=op0, op1=op1), R, W, "stt", cost=self._ec("dve", out) + out.free_size() * 1.0)

    def act(self, out, in_, func, R, W, bias=None, scale=None):
        kw = {}
        if bias is not None:
            kw["bias"] = bias
        if scale is not None:
            kw["scale"] = scale
        o = self._op("act", lambda e: e.activation(out=out, in_=in_, func=func, **kw), R, W, "act", cost=self._ec("act", out) + 90.0 * len(kw))
        if o is not None:
            o.tab = ACT_TAB.get(func)
        return o

    def cp(self, eng, out, in_, R, W):
        if eng == "act":
            return self.act(out, in_, AF.Copy, R, W)
        return self._op(eng, lambda e: e.tensor_copy(out=out, in_=in_), R, W, "cp", cost=self._ec(eng, out))

    def memset(self, eng, out, val, W):
        return self._op(eng, lambda e: e.memset(out, val), [], W, "memset", cost=self._ec(eng, out))

    def mm(self, out, lhsT, rhs, start, stop, R, W):
        rows = (lhsT.base_partition(), lhsT.partition_size())
        passes = 4.0 if lhsT.dtype == F32 else (2.0 if lhsT.dtype == F32R else 1.0)
        cost = 40.0 + max(64.0, out.free_size() * passes) / 2.2
        return self._op("pe", lambda e: e.matmul(out, lhsT, rhs, start=start, stop=stop), R, W, "mm", cost=cost, rows=rows)

    def tr(self, out, in_, ident, R, W):
        return self._op("pe", lambda e: e.transpose(out, in_, ident), R, W, "tr", cost=60.0 + out.free_size() * 2.0 / 2.2,
                        rows=(in_.base_partition(), in_.partition_size()))

    def scan(self, out, d0, d1, init, R, W):
        return self._op("dve", lambda e: e.tensor_tensor_scan(out=out, data0=d0, data1=d1, initial=init, op0=ALU.mult, op1=ALU.add), R, W, "scan", cost=150.0 + out.free_size() * 2.1)

    def load(self, buf, out, in_, R=()):
        return self.P.op("sp", lambda e: e.dma_start(out=out, in_=in_), _regs(R), _regs([buf]), chan=buf.chan, ndma=1, name="load",
                         cost=120.0, dma_t=2000.0 + out.partition_size() * out.free_size() * 4 / 180.0)

    def store(self, buf, out, in_, W=()):
        o = self.P.op("sp", lambda e: e.dma_start(out=out, in_=in_), _regs([buf]), _regs(W), chan=buf.chan, ndma=1, name="store",
                      cost=120.0, dma_t=2000.0 + in_.partition_size() * in_.free_size() * 4 / 180.0)
        if o is not None:
            self.stores.append(o)
        return o


def build_program(dbg=False):
    nc = bass.Bass("TRN2", target_bir_lowering=False, dynamic_dma_scratch_size=8192)
    shapes_in = {
        "xin": [NTOK, D], "st_pool": [SB * 15, 256], "st_shift": [SB, 2560], "st_wkv": [SB * 12 * 64, 64],
        "st_conv": [SB * 3, 512], "st_lru": [SB, 512],
        "ev_w_in": [D, 2816], "ev_w_out": [D, D], "od_w_in": [D, 2048], "od_w_out": [D, D],
        "ff_w1": [2, D, 4096], "ff_w2": [2, 4096, D],
    }
    for nm, ncol in CONST_COLS.items():
        shapes_in[nm] = [128, ncol]
    shapes_out = {
        "y": [NTOK, D], "o_pool_p": [15, 256], "o_shift_p": [1, 2560], "o_wkv_p": [768, 64], "o_conv_p": [3, 512],
        "o_lru_p": [1, 512], "o_pool_s": [SB * 15, 256], "o_shift_s": [SB, 2560], "o_wkv_s": [SB * 768, 64],
        "o_conv_s": [SB * 3, 512], "o_lru_s": [SB, 512], "o_gv_s": [NS, 512],
    }
    dr = {}
    for nm, sh in shapes_in.items():
        dr[nm] = nc.dram_tensor(nm, sh, F32, kind="ExternalInput").ap()
    for nm, sh in shapes_out.items():
        dr[nm] = nc.dram_tensor(nm, sh, F32, kind="ExternalOutput").ap()
    dr["xscr"] = nc.dram_tensor("xscr", [8, 128, NTOK], F32, kind="Internal").ap()
    K = KB(nc, dr)
    _emit_all(K)
    if SCHED:
        K.P.schedule()
    K.P.emit(final_wait_ops=K.stores)
    K.P.close()
    return nc


CONST_COLS = {
    "pcols": 0,
    "ident": 128, "bd1": 128, "ones": 128,
    "mN_p": 128, "m2_p": 256, "mN_s": 64, "m2_s": 128, "rm_p": 128, "rm_s": 64, "rowmask": 16,
    "invc0": 256, "invc1": 256, "wa_w2": 768, "g_w2": 768, "poolw_bd": 256, "wx_bd": 512, "wa_bd": 512,
    "wmT": 512, "bs_bc": 512, "wsb": 64,
}
PC = {}


def _pc_layout():
    names = [("ev_norm_g", 8), ("pool_scale", 2), ("mu", 20), ("w0", 6), ("a0", 6), ("k_k", 6), ("k_a", 6), ("r_k", 6),
             ("gn_g", 6), ("gn_b", 6), ("od_norm_g", 8), ("ln_g", 4), ("ln_b", 4), ("conv_w0", 4), ("conv_w1", 4),
             ("conv_w2", 4), ("conv_w3", 4), ("conv_b", 4), ("bx", 4), ("ba", 4), ("lam", 4), ("ff_g0", 8), ("ff_g1", 8),
             ("fin_g", 8), ("c_eps6", 1), ("c_epsgn", 1), ("c_epsln", 1), ("c_one", 1), ("c_tiny", 1)]
    off = 0
    for nm, n in names:
        PC[nm] = (off, n)
        off += n
    CONST_COLS["pcols"] = off


_pc_layout()


def _emit_all(K):
    P = K.P
    dr = K.dr
    C = {}
    for nm, ncol in CONST_COLS.items():
        C[nm] = K.alloc(ncol)
    first = True
    for nm in CONST_COLS:
        K.load(C[nm], C[nm].ap(0, CONST_COLS[nm]), dr[nm])

    def col(nm, j=0):
        o, n = PC[nm]
        return C["pcols"].ap(o + j, o + j + 1)

    ident = C["ident"]
    bd1 = C["bd1"]
    ones = C["ones"]
    if FILLERS:
        fsrc = K.alloc(192)
        K.cp("dve", fsrc.apb(0, 128), ones.ap(0, 128), [ones], [fsrc])
        K.cp("dve", fsrc.apb(128, 256), ones.ap(0, 128), [ones], [fsrc])
        fdep = K.cp("dve", fsrc.apb(256, 384), ident.ap(0, 128), [ident], [fsrc])
        f_out, f_l, f_r = K.psb[7].ap(0, 256), fsrc.apb(0, 128), fsrc.apb(128, 384)
        K.P.filler = lambda: K.P.make_filler(lambda e: e.matmul(f_out, f_l, f_r, start=True, stop=True), fdep)

    for h in range(4):
        K.tt("dve", C["wmT"].ap(h * 128, h * 128 + 128), C["wmT"].ap(h * 128, h * 128 + 128), C["m2_p"].ap(128, 256), ALU.mult,
             [C["wmT"], C["m2_p"]], [C["wmT"]])
    nsp8 = K.alloc(4)
    o_l, _ = PC["lam"]
    K.act(nsp8.ap(0, 4), C["pcols"].ap(o_l, o_l + 4), AF.Exp, [C["pcols"]], [nsp8], scale=-1.0)
    K.act(nsp8.ap(0, 4), nsp8.ap(0, 4), AF.Ln, [nsp8], [nsp8], bias=col("c_one"))
    K.ts("dve", nsp8.ap(0, 4), nsp8.ap(0, 4), -8.0, None, ALU.mult, None, [nsp8], [nsp8])

    EP = [K.alloc(304) for _ in range(2)]
    ES = K.alloc(20 * 129)
    EC = [K.alloc(3 + BN) for _ in range(4)]
    HL = [K.alloc(16) for _ in range(4)]
    Hp = K.alloc(6 * 64)
    for b in EP + EC + HL + [ES, Hp]:
        K.memset("pool", b.ap(), 0.0, [b])

    NWS = 6
    WB = [K.alloc(8 * 128) for _ in range(NWS)]

    def wtile(nk, cw, src3):
        i = wsl[0] % NWS
        wsl[0] += 1
        wb = WB[i]
        dst = wb.apb(0, nk * cw).rearrange("p (k c) -> p k c", c=cw)
        K.P.op("pool", lambda e: e.dma_start(out=dst, in_=src3), [], _regs([wb]), chan=wb.chan, ndma=1, name="wload",
               cost=1000.0, dma_t=2500.0 + 128 * nk * cw * 4 / 160.0)
        return wb

    stg_in = [K.alloc(1024) for _ in range(1)]
    stg_out = [K.alloc(1024) for _ in range(1)]
    hb = [K.alloc(BN // 2) for _ in range(8)]
    ym = [K.alloc(BN // 2) for _ in range(8)]
    pr = [K.alloc(BN) for _ in range(22)]
    base_top = K.top
    xb = [K.alloc(BN) for _ in range(8)]
    regA = [hb[0].off, base_top]

    def allocA(n):
        n = (n + GRAN - 1) // GRAN * GRAN
        if regA[0] + n <= regA[1]:
            b = Buf(K, regA[0], n)
            regA[0] += n
            return b
        return K.alloc(n)

    wsl = [0]

    def wslot():
        b = WB[wsl[0] % 2]
        wsl[0] += 1
        return b

    so = [0]

    def sout():
        b = stg_out[0]
        so[0] += 1
        return b

    si = [0]

    def sin():
        b = stg_in[0]
        si[0] += 1
        return b

    def fm2tm_store(srcs, m, dst_ap):
        k = len(srcs)
        st = sout()
        for g0 in range(0, k, 4):
            g = srcs[g0:g0 + 4]
            ps = K.ps()
            for i, (a, bufs) in enumerate(g):
                K.tr(ps.ap(i * 128, i * 128 + 128, 0, m), a, ident.ap(0, 128), bufs + [ident], [ps])
            K.cp(K.ev_eng(), st.ap(g0 * 128, (g0 + len(g)) * 128, 0, m), ps.ap(0, len(g) * 128, 0, m), [ps], [st])
        K.store(st, dst_ap, st.ap(0, k * 128, 0, m))

    def tm2fm_load(src_ap, m, k, dsts):
        st = sin()
        K.load(st, st.ap(0, k * 128, 0, m), src_ap)
        for g0 in range(0, k, 4):
            g = dsts[g0:g0 + 4]
            ps = K.ps()
            for i in range(len(g)):
                K.tr(ps.ap(i * m, i * m + m), st.ap((g0 + i) * 128, (g0 + i + 1) * 128, 0, m), ident.ap(0, m, 0, m), [st, ident], [ps])
            for i, (oa, bufs, shp) in enumerate(g):
                src = ps.ap(i * m, i * m + m)
                if shp is not None:
                    src = src.rearrange("p (a b) -> p a b", b=shp)
                K.cp(K.ev_eng(), oa, src, [ps], bufs)

    def rmsnorm(src, gname, dst, n, epsname="c_eps6", bf=True):
        m0 = K.top
        sq = [K.alloc(n) for _ in range(1)]
        rs = K.alloc(n)
        ps = K.ps()
        for k in range(8):
            s = sq[0]
            K.act(s.ap(0, n), src[k].ap(0, n), AF.Square, [src[k]], [s])
            K.mm(ps.ap(0, n), ones.ap(0, 128), s.ap(0, n), k == 0, k == 7, [ones, s], [ps])
        K.act(rs.ap(0, n), ps.ap(0, n), AF.Ln, [ps], [rs], bias=col(epsname), scale=1.0 / D)
        K.act(rs.ap(0, n), rs.ap(0, n), AF.Exp, [rs], [rs], scale=-0.5)
        for k in range(8):
            K.stt(dst[k].apb(0, n) if bf else dst[k].ap(0, n), src[k].ap(0, n), col(gname, k), rs.ap(0, n), ALU.mult, ALU.mult, [src[k], rs, C["pcols"]], [dst[k]])
        K.top = m0

    def linear(wname, lidx, K_tiles, ncols_out, n, sink):
        w = dr[wname]
        if lidx is not None:
            w = w[lidx]
        nk = len(K_tiles)
        for c0 in range(0, ncols_out, 256):
            cw = min(256, ncols_out - c0)
            wb = wtile(nk, cw, w[:, c0:c0 + cw].rearrange("(k p) c -> p k c", p=128))
            for mi in range(cw // 128):
                ps = K.ps()
                for k in range(nk):
                    a, bufs = K_tiles[k]
                    K.mm(ps.ap(0, n), wb.apb(k * cw + mi * 128, k * cw + mi * 128 + 128), a, k == 0, k == nk - 1, [wb] + bufs, [ps])
                sink(c0 // 128 + mi, ps)

    TBS = [(t0, min(512, NTOK - t0)) for t0 in range(0, NTOK, 512)]

    def ffn_all(layer, xall):
        m0 = K.top
        regA[0] = hb[0].off
        hf = [allocA(NTOK // 2) for _ in range(8)]
        acs = [[allocA(NTOK // 2) for _ in range(2)] for _ in range(1)]
        tmp = [allocA(512) for _ in range(2)]
        for (t0, tn) in TBS:
            rmsnorm([xall[k].sub(t0, tn) for k in range(8)], "ff_g%d" % layer, [hf[k].sub(t0 // 2, tn // 2) for k in range(8)], tn)
        w1 = dr["ff_w1"][layer]
        w2 = dr["ff_w2"][layer]
        ti = 0
        for c in range(16):
            ac = acs[0]
            wb1 = wtile(8, 256, w1[:, c * 256:(c + 1) * 256].rearrange("(k p) c -> p k c", p=128))
            for mi in range(2):
                for (t0, tn) in TBS:
                    ps = K.ps()
                    for k in range(8):
                        hk = hf[k].sub(t0 // 2, tn // 2)
                        K.mm(ps.ap(0, tn), wb1.apb(k * 256 + mi * 128, k * 256 + mi * 128 + 128), hk.apb(0, tn), k == 0, k == 7, [wb1, hk], [ps])
                    t = tmp[ti % 2]
                    ti += 1
                    K.act(t.ap(0, tn), ps.ap(0, tn), AF.Relu, [ps], [t])
                    ak = ac[mi].sub(t0 // 2, tn // 2)
                    K.tt("pool", ak.apb(0, tn), t.ap(0, tn), t.ap(0, tn), ALU.mult, [t], [ak])
            wb2 = wtile(2, 1024, w2[c * 256:(c + 1) * 256, :].rearrange("(k p) c -> p k c", p=128))
            for m in range(8):
                for (t0, tn) in TBS:
                    ps = K.ps()
                    for k in range(2):
                        ak = ac[k].sub(t0 // 2, tn // 2)
                        K.mm(ps.ap(0, tn), wb2.apb(k * 1024 + m * 128, k * 1024 + m * 128 + 128), ak.apb(0, tn), k == 0, k == 1, [wb2, ak], [ps])
                    xm = xall[m].sub(t0, tn)
                    K.tt("dve", xm.ap(0, tn), xm.ap(0, tn), ps.ap(0, tn), ALU.add, [xm, ps], [xm])
        K.top = m0

    def v3(buf, nseq, width, c0, ln, p0=0, p1=128):
        return K.arena[p0:p1, buf.off:buf.off + nseq * width].rearrange("p (s w) -> p s w", w=width)[:, :, c0:c0 + ln]

    def pool_mixer(ch, o):
        n, nseq, L = ch["n"], ch["nseq"], ch["L"]
        W = 15 + L
        m0 = K.top
        S = [K.alloc(304) for _ in range(4)]
        Dd = K.alloc(128)
        invc = C["invc0"] if (ch["first"] and not ch["sample"]) else C["invc1"]
        for j in range(2):
            e = EP[j]
            if not ch["first"]:
                K.cp("pool", v3(e, nseq, W, 0, 15), v3(e, nseq, W, L, 15), [e], [e])
            K.cp("act", v3(e, nseq, W, 15, L), pr[j].ap(o, o + n).rearrange("p (s l) -> p s l", l=L), [pr[j]], [e])
            prev = e
            shifts = [1, 2, 4, 8] if j == 1 else [1, 2]
            lo = 0
            for si_, sh in enumerate(shifts):
                lo2 = lo + sh
                K.tt("dve" if si_ % 2 == 0 else "pool", v3(S[si_], nseq, W, lo2, W - lo2), v3(prev, nseq, W, lo2, W - lo2), v3(prev, nseq, W, lo, W - lo2), ALU.add,
                     [prev], [S[si_]])
                prev = S[si_]
                lo = lo2
            slo, shi = (S[0], S[1]) if j == 0 else (S[2], S[3])
            dv = Dd.ap(0, n).rearrange("p (s l) -> p s l", l=L)
            for (p0, p1, sbuf_) in ((0, 64, slo), (64, 128, shi)):
                K.tt("dve", Dd.ap(0, n, p0, p1).rearrange("p (s l) -> p s l", l=L), v3(sbuf_, nseq, W, 15, L, p0, p1),
                     invc.ap(j * 128, j * 128 + n, p0, p1).rearrange("p (s l) -> p s l", l=L), ALU.mult, [sbuf_, invc], [Dd])
            K.tt("dve", dv, dv, v3(e, nseq, W, 15, L), ALU.subtract, [Dd, e], [Dd])
            ps = K.ps()
            K.mm(ps.ap(0, n), C["poolw_bd"].ap(j * 128, j * 128 + 128), Dd.ap(0, n), True, True, [C["poolw_bd"], Dd], [ps])
            K.ts("dve", ym[j].apb(o, o + n), ps.ap(0, n), col("pool_scale", j), None, ALU.mult, None, [ps, C["pcols"]], [ym[j]])
        if ch["last"]:
            stc = [K.alloc(256) for _ in range(2)]
            for j in range(2):
                K.cp("pool", stc[j].ap(0, nseq * 15).rearrange("p (s l) -> p s l", l=15), v3(EP[j], nseq, W, L, 15), [EP[j]], [stc[j]])
            dst = dr["o_pool_s"] if ch["sample"] else dr["o_pool_p"]
            tot = nseq * 15
            piece = 120 if tot > 128 else tot
            for r0 in range(0, tot, piece):
                fm2tm_store([(stc[j].ap(r0, r0 + piece), [stc[j]]) for j in range(2)], piece, dst[r0:r0 + piece, :])
        K.top = m0

    def rwkv(ch, o):
        n, nseq, L = ch["n"], ch["nseq"], ch["L"]
        sample = ch["sample"]
        Wd = 1 + L
        m0 = K.top
        XS = [K.alloc(128) for _ in range(20)]
        mN = C["mN_s"] if sample else C["mN_p"]
        m2 = C["m2_s"] if sample else C["m2_p"]
        rm = C["rm_s"] if sample else C["rm_p"]
        ESq = [ES.sub(q * 129, 129) for q in range(20)]
        esv = lambda q, c0, ln: K.arena[:, ES.off + q * 129: ES.off + q * 129 + nseq * Wd].rearrange("p (s w) -> p s w", w=Wd)[:, :, c0:c0 + ln]
        dtmp = [K.alloc(128) for _ in range(8)]
        for q in range(20):
            if not ch["first"]:
                K.cp("pool", esv(q, 0, 1), esv(q, L, 1), [ESq[q]], [ESq[q]])
        for q in range(20):
            K.cp("act" if q % 2 else "pool", esv(q, 1, L), pr[2 + q].ap(o, o + n).rearrange("p (s l) -> p s l", l=L), [pr[2 + q]], [ESq[q]])
        for q in range(20):
            d = dtmp[q % 8]
            dv = d.ap(0, n).rearrange("p (s l) -> p s l", l=L)
            K.tt("pool" if q % 2 else "dve", dv, esv(q, 0, L), esv(q, 1, L), ALU.subtract, [ESq[q]], [d])
            K.stt(XS[q].ap(0, n).rearrange("p (s l) -> p s l", l=L), dv, col("mu", q), esv(q, 1, L), ALU.mult, ALU.add, [d, ESq[q], C["pcols"]], [XS[q]])
        if ch["last"]:
            stc = K.alloc(20 * 16)
            K.cp("pool", stc.ap(0, 20 * nseq).rearrange("p (q s) -> p q s", s=nseq),
                 K.arena[:, ES.off:ES.off + 20 * 129].rearrange("p (q w) -> p q w", w=129)[:, :, 0:nseq * Wd].rearrange("p q (s w) -> p q s w", w=Wd)[:, :, :, L],
                 [ES], [stc])
            dst = dr["o_shift_s"] if sample else dr["o_shift_p"]
            for g0 in range(0, 20, 8):
                g = list(range(g0, min(20, g0 + 8)))
                fm2tm_store([(stc.ap(q * nseq, q * nseq + nseq), [stc]) for q in g], nseq, dst[:, g0 * 128:(g0 + len(g)) * 128])
        Rr = XS[0:6]
        Kk = XS[6:12]
        Vv = XS[12:18]
        TH = K.alloc(128)
        SGg = K.alloc(128)
        K.act(TH.ap(0, n, 0, 64), XS[18].ap(0, n, 0, 64), AF.Tanh, [XS[18]], [TH])
        K.act(SGg.ap(0, n), XS[19].ap(0, n), AF.Sigmoid, [XS[19]], [SGg])
        LW = [K.alloc(128) for _ in range(6)]
        Aa = [K.alloc(128) for _ in range(6)]
        Gg = [K.alloc(128) for _ in range(6)]
        for j in range(6):
            ps = K.ps()
            K.mm(ps.ap(0, n), C["wa_w2"].ap(j * 128, j * 128 + 128, 0, 64), TH.ap(0, n, 0, 64), True, True, [C["wa_w2"], TH], [ps])
            K.act(LW[j].ap(0, n), ps.ap(0, n), AF.Sigmoid, [ps, C["pcols"]], [LW[j]], bias=col("w0", j))
            ps = K.ps()
            K.mm(ps.ap(0, n), C["wa_w2"].ap(j * 128, j * 128 + 128, 64, 128), XS[18].ap(0, n, 64, 128), True, True, [C["wa_w2"], XS[18]], [ps])
            K.act(Aa[j].ap(0, n), ps.ap(0, n), AF.Sigmoid, [ps, C["pcols"]], [Aa[j]], bias=col("a0", j))
            ps = K.ps()
            K.mm(ps.ap(0, n), C["g_w2"].ap(j * 128, j * 128 + 128), SGg.ap(0, n), True, True, [C["g_w2"], SGg], [ps])
            K.cp("dve", Gg[j].ap(0, n), ps.ap(0, n), [ps], [Gg[j]])
        KK = [K.alloc(128) for _ in range(6)]
        KP = [K.alloc(128) for _ in range(6)]
        BON = [K.alloc(128) for _ in range(6)]
        t1 = [K.alloc(128) for _ in range(6)]
        t2 = [K.alloc(128) for _ in range(6)]
        t3 = [K.alloc(128) for _ in range(6)]
        for j in range(6):
            a, b = t1[j], t2[j]
            K.ts("dve", KK[j].ap(0, n), Kk[j].ap(0, n), col("k_k", j), None, ALU.mult, None, [Kk[j], C["pcols"]], [KK[j]])
            K.tt("pool", a.ap(0, n), KK[j].ap(0, n), KK[j].ap(0, n), ALU.mult, [KK[j]], [a])
            ps = K.ps()
            K.mm(ps.ap(0, n), bd1.ap(0, 128), a.ap(0, n), True, True, [bd1, a], [ps])
            K.act(b.ap(0, n), ps.ap(0, n), AF.Ln, [ps, C["pcols"]], [b], bias=col("c_tiny"))
            K.act(b.ap(0, n), b.ap(0, n), AF.Exp, [b], [b], scale=-0.5)
            K.tt("dve", KK[j].ap(0, n), KK[j].ap(0, n), b.ap(0, n), ALU.mult, [KK[j], b], [KK[j]])
            K.ts("dve", a.ap(0, n), Aa[j].ap(0, n), -1.0, col("k_a", j), ALU.add, ALU.mult, [Aa[j], C["pcols"]], [a])
            K.stt(KP[j].ap(0, n), a.ap(0, n), 1.0, Kk[j].ap(0, n), ALU.add, ALU.mult, [a, Kk[j]], [KP[j]])
            K.stt(a.ap(0, n), Rr[j].ap(0, n), col("r_k", j), KP[j].ap(0, n), ALU.mult, ALU.mult, [Rr[j], KP[j], C["pcols"]], [a])
            ps = K.ps()
            K.mm(ps.ap(0, n), bd1.ap(0, 128), a.ap(0, n), True, True, [bd1, a], [ps])
            K.tt("dve", BON[j].ap(0, n), ps.ap(0, n), Vv[j].ap(0, n), ALU.mult, [ps, Vv[j]], [BON[j]])
            K.ts("dve", BON[j].ap(0, n), BON[j].ap(0, n), col("gn_b", j), None, ALU.add, None, [BON[j], C["pcols"]], [BON[j]])
        LP = [K.alloc(128) for _ in range(6)]
        AR = [K.alloc(256) for _ in range(6)]
        BT = KK
        KT = Kk
        PCc = [K.alloc(16) for _ in range(6)]
        for j in range(6):
            e = t1[j]
            CD = -0.6065306597126334
            K.scan(LP[j].ap(0, n), rm.ap(0, n), LW[j].ap(0, n), 0.0, [rm, LW[j]], [LP[j]])
            K.act(e.ap(0, n), LP[j].ap(0, n), AF.Exp, [LP[j]], [e], scale=CD)
            K.tt("pool", AR[j].ap(n, 2 * n), Rr[j].ap(0, n), e.ap(0, n), ALU.mult, [Rr[j], e], [AR[j]])
            K.act(PCc[j].ap(0, nseq), LP[j].ap(0, n).rearrange("p (s l) -> p s l", l=L)[:, :, L - 1], AF.Exp, [LP[j]], [PCc[j]], scale=CD)
            e2 = t2[j]
            K.tt("dve", e2.ap(0, n), LP[j].ap(0, n), LW[j].ap(0, n), ALU.subtract, [LP[j], LW[j]], [e2])
            K.act(e2.ap(0, n), e2.ap(0, n), AF.Exp, [e2], [e2], scale=CD)
            K.stt(AR[j].ap(0, n), KK[j].ap(0, n), -1.0, e2.ap(0, n), ALU.mult, ALU.mult, [KK[j], e2], [AR[j]])
            e3 = t3[j]
            K.act(e3.ap(0, n), LP[j].ap(0, n), AF.Exp, [LP[j]], [e3], scale=-CD)
            K.tt("pool", BT[j].ap(0, n), KK[j].ap(0, n), Aa[j].ap(0, n), ALU.mult, [KK[j], Aa[j]], [BT[j]])
            K.tt("dve", BT[j].ap(0, n), BT[j].ap(0, n), e3.ap(0, n), ALU.mult, [BT[j], e3], [BT[j]])
            K.tt("pool", KT[j].ap(0, n), KP[j].ap(0, n), e3.ap(0, n), ALU.mult, [KP[j], e3], [KT[j]])
        TM = [K.alloc(768) for _ in range(3)]
        for qi, srcl in enumerate((Vv, BT, KT)):
            for g0 in (0, 4):
                g = list(range(g0, min(6, g0 + 4)))
                ps = K.ps()
                for i, j in enumerate(g):
                    K.tr(ps.ap(i * 128, i * 128 + 128, 0, n), srcl[j].ap(0, n), ident.ap(0, 128), [srcl[j], ident], [ps])
                K.cp(K.ev_eng(), TM[qi].ap(g0 * 128, (g0 + len(g)) * 128, 0, n), ps.ap(0, len(g) * 128, 0, n), [ps], [TM[qi]])
        Vtm, Btm, Ktm = TM
        if sample:
            assert nseq == 16
            assert WB[NWS - 1].off + WB[NWS - 1].n - WB[0].off == 16 * 384
            Hs = [Buf(K, WB[0].off + s * 384, 384) for s in range(16)]
            stw = dr["st_wkv"].rearrange("(s h v) k -> s v h k", h=12, v=64)
            for s in range(nseq):
                st = sin()
                K.load(st, st.ap(0, 768, 0, 64).rearrange("p (h k) -> p h k", k=64), stw[s])
                ps = K.ps()
                for j in range(6):
                    K.tr(ps.ap(j * 64, j * 64 + 64), st.ap(j * 128, j * 128 + 128, 0, 64), ident.ap(0, 64, 0, 64), [st, ident], [ps])
                K.cp(K.ev_eng(), Hs[s].ap(0, 384), ps.ap(0, 384), [ps], [Hs[s]])
        else:
            Hs = [Hp]
        OT = Rr
        Ut = K.alloc(768)
        nlev = int(math.ceil(math.log2(L)))
        GN = K.alloc(4 * n)
        G2 = K.alloc(8 * n)
        G3 = K.alloc(8 * n)
        if sample:
            Pb = [K.alloc(4 * n) for _ in range(2)]
            Ptb = [K.alloc(4 * n) for _ in range(2)]
            Tt = K.alloc(4 * n)
        else:
            PTb = [Buf(K, i_ * 1024, 1024, ten=K.arena2, gr=K.gran2) for i_ in range(2)]
            QTb = [Buf(K, 2048 + i_ * 1024, 1024, ten=K.arena2, gr=K.gran2) for i_ in range(2)]
        r32 = lambda b_, c0_, c1_: b_.ten[:, b_.off + c0_:b_.off + c1_]
        slot = lambda b_, h0_, nh_, w_: b_.ten[:, b_.off + h0_ * 256:b_.off + (h0_ + nh_) * 256].rearrange("p (h c) -> p h c", c=256)[:, :, w_ * 128:(w_ + 1) * 128]
        X0T = K.alloc(256)
        X0 = K.alloc(256)
        Xs = K.alloc(256)
        UVms = [K.alloc(512) for _ in range(1)] if sample else None
        uvi = [0]
        for hg in range(3):
            heads = [4 * hg + i for i in range(4)]
            psAe, psAo = K.ps(), K.ps()
            psB1, psB2 = K.ps(), K.ps()
            psC1, psC2 = K.ps(), K.ps()
            for hh, h in enumerate(heads):
                j, b = h // 2, 64 * (h % 2)
                aT = AR[j].ap(0, n, b, b + 64)
                arT = AR[j].ap(0, 2 * n, b, b + 64)
                bT = BT[j].ap(0, n, b, b + 64)
                kT = KT[j].ap(0, n, b, b + 64)
                psA = psAe if hh % 2 == 0 else psAo
                K.mm(psA.ap(hh * n, hh * n + n, 0, n), aT, bT, True, True, [AR[j], BT[j]], [psA])
                pB = psB1 if hh % 2 == 0 else psB2
                pC = psC1 if hh % 2 == 0 else psC2
                c0 = (hh // 2) * 2 * n
                K.mm(pB.ap(c0, c0 + 2 * n, 0, n), bT, arT, True, True, [AR[j], BT[j]], [pB])
                K.mm(pC.ap(c0, c0 + 2 * n, 0, n), kT, arT, True, True, [AR[j], KT[j]], [pC])
            for hh in range(4):
                psA = psAe if hh % 2 == 0 else psAo
                K.tt("dve", GN.ap(hh * n, hh * n + n, 0, n), psA.ap(hh * n, hh * n + n, 0, n), mN.ap(0, n, 0, n), ALU.mult, [psA, mN], [GN])
                pB = psB1 if hh % 2 == 0 else psB2
                pC = psC1 if hh % 2 == 0 else psC2
                c0 = (hh // 2) * 2 * n
                K.tt("dve", G2.ap(hh * 2 * n, hh * 2 * n + 2 * n, 0, n), pB.ap(c0, c0 + 2 * n, 0, n), m2.ap(0, 2 * n, 0, n), ALU.mult, [pB, m2], [G2])
                K.tt("dve", G3.ap(hh * 2 * n, hh * 2 * n + 2 * n, 0, n), pC.ap(c0, c0 + 2 * n, 0, n), m2.ap(0, 2 * n, 0, n), ALU.mult, [pC, m2], [G3])
            if sample:
                Pc, Ptc = Pb[0], Ptb[0]
                for hh in range(4):
                    K.cp("pool", Pc.ap(hh * n, hh * n + n, 0, n), GN.ap(hh * n, hh * n + n, 0, n), [GN], [Pc])
                    K.cp("pool", Ptc.ap(hh * n, hh * n + n, 0, n), G2.ap(hh * 2 * n, hh * 2 * n + n, 0, n), [G2], [Ptc])
                    K.tt("dve", Tt.ap(hh * n, hh * n + n, 0, n), G2.ap(hh * 2 * n, hh * 2 * n + n, 0, n), ident.ap(0, n, 0, n), ALU.add, [G2, ident], [Tt])
                for lev in range(1, nlev):
                    Pn, Ptn = Pb[lev % 2], Ptb[lev % 2]
                    ps1 = K.ps()
                    for hh in range(4):
                        K.mm(ps1.ap(hh * n, hh * n + n, 0, n), Ptc.ap(hh * n, hh * n + n, 0, n), Pc.ap(hh * n, hh * n + n, 0, n), True, True, [Pc, Ptc], [ps1])
                    K.cp("act", Pn.ap(0, 4 * n, 0, n), ps1.ap(0, 4 * n, 0, n), [ps1], [Pn])
                    if lev < nlev - 1:
                        ps2 = K.ps()
                        for hh in range(4):
                            K.mm(ps2.ap(hh * n, hh * n + n, 0, n), Pc.ap(hh * n, hh * n + n, 0, n), Ptc.ap(hh * n, hh * n + n, 0, n), True, True, [Pc, Ptc], [ps2])
                        K.cp("act", Ptn.ap(0, 4 * n, 0, n), ps2.ap(0, 4 * n, 0, n), [ps2], [Ptn])
                    ps3 = K.ps()
                    for hh in range(4):
                        K.mm(ps3.ap(hh * n, hh * n + n, 0, n), Pn.ap(hh * n, hh * n + n, 0, n), Tt.ap(hh * n, hh * n + n, 0, n), True, True, [Pn, Tt], [ps3])
                    K.tt("dve", Tt.ap(0, 4 * n, 0, n), Tt.ap(0, 4 * n, 0, n), ps3.ap(0, 4 * n, 0, n), ALU.add, [Tt, ps3], [Tt])
                    Pc, Ptc = Pn, Ptn

                tt_ap = lambda hh: Tt.ap(hh * n, hh * n + n, 0, n)
            else:
                for bi_, bdst in enumerate((PTb[0], QTb[0])):
                    for hh in range(4):
                        K.cp("pool" if (hh + bi_) % 2 else "dve", r32(bdst, hh * 256 + 128, hh * 256 + 256), ident.ap(0, 128), [ident], [bdst])
                K.cp("act", slot(PTb[0], 0, 4, 0), GN.ap(0, 512).rearrange("p (h c) -> p h c", c=128), [GN], [PTb[0]])
                K.cp("act", slot(QTb[0], 0, 4, 0), G2.ap(0, 1024).rearrange("p (h c) -> p h c", c=256)[:, :, 0:128], [G2], [QTb[0]])
                NLV = 7
                for lev in range(NLV):
                    cur, nx = lev % 2, (lev + 1) % 2
                    last = lev == NLV - 1
                    for pp in range(2):
                        if not last:
                            psA_ = K.ps()
                            for h2 in range(2):
                                hh = 2 * pp + h2
                                K.mm(psA_.ap(h2 * 256, h2 * 256 + 256), r32(QTb[cur], hh * 256, hh * 256 + 128), r32(PTb[cur], hh * 256, hh * 256 + 256), True, True, [QTb[cur], PTb[cur]], [psA_])
                            pv = psA_.ap(0, 512).rearrange("p (h c) -> p h c", c=256)
                            K.cp("act", slot(PTb[nx], 2 * pp, 2, 0), pv[:, :, 0:128], [psA_], [PTb[nx]])
                            K.tt("dve", slot(PTb[nx], 2 * pp, 2, 1), slot(PTb[cur], 2 * pp, 2, 1).bitcast(F32), pv[:, :, 128:256], ALU.add, [psA_, PTb[cur]], [PTb[nx]])
                        psB_ = K.ps()
                        for h2 in range(2):
                            hh = 2 * pp + h2
                            K.mm(psB_.ap(h2 * 256, h2 * 256 + 256), r32(PTb[cur], hh * 256, hh * 256 + 128), r32(QTb[cur], hh * 256, hh * 256 + 256), True, True, [QTb[cur], PTb[cur]], [psB_])
                        pv = psB_.ap(0, 512).rearrange("p (h c) -> p h c", c=256)
                        if not last:
                            K.cp("act", slot(QTb[nx], 2 * pp, 2, 0), pv[:, :, 0:128], [psB_], [QTb[nx]])
                        K.tt("dve", slot(QTb[nx], 2 * pp, 2, 1), slot(QTb[cur], 2 * pp, 2, 1).bitcast(F32), pv[:, :, 128:256], ALU.add, [psB_, QTb[cur]], [QTb[nx]])
                QF = QTb[NLV % 2]
                Tt = QF
                tt_ap = lambda hh, QF=QF: QF.ap(hh * 256 + 128, hh * 256 + 256).bitcast(F32)
            psXb = {0: K.ps(), 64: K.ps()}
            for b in (0, 64):
                psX = psXb[b]
                for jj in range(2):
                    j = 2 * hg + jj
                    for s in range(nseq):
                        K.mm(psX.ap(jj * n + s * L, jj * n + s * L + L, b, b + 64), Hs[s].ap(j * 64, j * 64 + 64, b, b + 64),
                             AR[j].ap(s * L, s * L + L, b, b + 64), True, True, [Hs[s], AR[j]], [psX])
                K.cp("act" if b == 0 else "dve", X0T.ap(0, 2 * n, b, b + 64), psX.ap(0, 2 * n, b, b + 64), [psX], [X0T])
            psXt = K.ps()
            for jj in range(2):
                K.tr(psXt.ap(jj * 128, jj * 128 + 128, 0, n), X0T.ap(jj * n, jj * n + n), ident.ap(0, 128), [X0T, ident], [psXt])
            K.cp("dve", X0.ap(0, 256, 0, n), psXt.ap(0, 256, 0, n), [psXt], [X0])
            psX2 = K.ps()
            for hh, h in enumerate(heads):
                K.mm(psX2.ap(hh * 64, hh * 64 + 64, 0, n), G3.ap(hh * 2 * n, hh * 2 * n + n, 0, n), Vtm.ap(h * 64, h * 64 + 64, 0, n), True, True, [G3, Vtm], [psX2])
            K.tt("dve", Xs.ap(0, 256, 0, n), psX2.ap(0, 256, 0, n), X0.ap(0, 256, 0, n), ALU.add, [psX2, X0], [Xs])
            psU = K.ps()
            for hh, h in enumerate(heads):
                K.mm(psU.ap(hh * 64, hh * 64 + 64, 0, n), tt_ap(hh), Xs.ap(hh * 64, hh * 64 + 64, 0, n), True, True, [Tt, Xs], [psU])
            K.cp("act", Ut.ap(hg * 256, hg * 256 + 256, 0, n), psU.ap(0, 256, 0, n), [psU], [Ut])
            psOb = {0: K.ps(), 64: K.ps()}
            for hh, h in enumerate(heads):
                j, b = h // 2, 64 * (h % 2)
                jj = hh // 2
                psO = psOb[b]
                oa = psO.ap(jj * n, jj * n + n, b, b + 64)
                K.mm(oa, Ut.ap(h * 64, h * 64 + 64, 0, n), G2.ap(hh * 2 * n + n, hh * 2 * n + 2 * n, 0, n), True, False, [Ut, G2], [psO])
                K.mm(oa, Vtm.ap(h * 64, h * 64 + 64, 0, n), G3.ap(hh * 2 * n + n, hh * 2 * n + 2 * n, 0, n), False, False, [Vtm, G3], [psO])
                for s in range(nseq):
                    K.mm(psO.ap(jj * n + s * L, jj * n + s * L + L, b, b + 64), Hs[s].ap(j * 64, j * 64 + 64, b, b + 64),
                         AR[j].ap(n + s * L, n + s * L + L, b, b + 64), False, s == nseq - 1, [Hs[s], AR[j]], [psO])
            for jj in range(2):
                for b in (0, 64):
                    K.cp("act" if b == 0 else "dve", OT[2 * hg + jj].ap(0, n, b, b + 64), psOb[b].ap(jj * n, jj * n + n, b, b + 64), [psOb[b]], [OT[2 * hg + jj]])
            for s in range(nseq):
                if nseq > 1:
                    UVm = UVms[0]
                    uvi[0] += 1
                    K.ts("dve", UVm.ap(0, 256, 0, n), Ut.ap(hg * 256, hg * 256 + 256, 0, n), C["rowmask"].ap(s, s + 1, 0, n), None, ALU.mult, None, [Ut, C["rowmask"]], [UVm])
                    K.act(UVm.ap(256, 512, 0, n), Vtm.ap(hg * 256, hg * 256 + 256, 0, n), AF.Copy, [Vtm, C["rowmask"]], [UVm], scale=C["rowmask"].ap(s, s + 1, 0, n))
                    ua = lambda hh: UVm.ap(hh * 64, hh * 64 + 64, 0, n)
                    va = lambda hh: UVm.ap(256 + hh * 64, 256 + hh * 64 + 64, 0, n)
                    ub = [UVm]
                else:
                    ua = lambda hh: Ut.ap((4 * hg + hh) * 64, (4 * hg + hh) * 64 + 64, 0, n)
                    va = lambda hh: Vtm.ap((4 * hg + hh) * 64, (4 * hg + hh) * 64 + 64, 0, n)
                    ub = [Ut, Vtm]
                psH = K.ps()
                for hh, h in enumerate(heads):
                    j, b = h // 2, 64 * (h % 2)
                    jj = hh // 2
                    oa = psH.ap(jj * 64, jj * 64 + 64, b, b + 64)
                    K.mm(oa, Btm.ap(h * 64, h * 64 + 64, 0, n), ua(hh), True, False, [Btm] + ub, [psH])
                    K.mm(oa, Ktm.ap(h * 64, h * 64 + 64, 0, n), va(hh), False, True, [Ktm] + ub, [psH])
                for jj in range(2):
                    j = 2 * hg + jj
                    hsub = Hs[s].sub(j * 64, 64)
                    K.ts("dve", hsub.ap(0, 64), hsub.ap(0, 64), PCc[j].ap(s, s + 1), None, ALU.mult, None, [hsub, PCc[j]], [hsub])
                    K.stt(hsub.ap(0, 64), psH.ap(jj * 64, jj * 64 + 64), PCc[j].ap(s, s + 1), hsub.ap(0, 64), ALU.mult, ALU.add, [psH, PCc[j], hsub], [hsub])
        if ch["last"]:
            dst = (dr["o_wkv_s"] if sample else dr["o_wkv_p"]).rearrange("(s h v) k -> s v h k", h=12, v=64)
            for s in range(nseq):
                st = sout()
                for g0 in (0, 4):
                    g = list(range(g0, min(6, g0 + 4)))
                    ps = K.ps()
                    for i, j in enumerate(g):
                        K.tr(ps.ap(i * 128, i * 128 + 128, 0, 64), Hs[s].ap(j * 64, j * 64 + 64), ident.ap(0, 128), [Hs[s], ident], [ps])
                    K.cp(K.ev_eng(), st.ap(g0 * 128, (g0 + len(g)) * 128, 0, 64), ps.ap(0, len(g) * 128, 0, 64), [ps], [st])
                K.store(st, dst[s], st.ap(0, 768, 0, 64).rearrange("p (h k) -> p h k", k=64))
        for j in range(6):
            a, b = t1[j], t2[j]
            ps = K.ps()
            K.mm(ps.ap(0, n), bd1.ap(0, 128), OT[j].ap(0, n), True, True, [bd1, OT[j]], [ps])
            K.stt(a.ap(0, n), ps.ap(0, n), -1.0 / 64, OT[j].ap(0, n), ALU.mult, ALU.add, [ps, OT[j]], [a])
            K.tt("pool", b.ap(0, n), a.ap(0, n), a.ap(0, n), ALU.mult, [a], [b])
            ps = K.ps()
            K.mm(ps.ap(0, n), bd1.ap(0, 128), b.ap(0, n), True, True, [bd1, b], [ps])
            K.act(b.ap(0, n), ps.ap(0, n), AF.Ln, [ps, C["pcols"]], [b], bias=col("c_epsgn"), scale=1.0 / 64)
            K.act(b.ap(0, n), b.ap(0, n), AF.Exp, [b], [b], scale=-0.5)
            K.stt(a.ap(0, n), a.ap(0, n), col("gn_g", j), b.ap(0, n), ALU.mult, ALU.mult, [a, b, C["pcols"]], [a])
            K.tt("dve", a.ap(0, n), a.ap(0, n), BON[j].ap(0, n), ALU.add, [a, BON[j]], [a])
            K.tt("dve", ym[2 + j].apb(o, o + n), a.ap(0, n), Gg[j].ap(0, n), ALU.mult, [a, Gg[j]], [ym[2 + j]])
        K.top = m0

    def gelu(dst_ap, src_ap, n, tmp, R, Wb):
        K.act(dst_ap, src_ap, AF.Gelu_apprx_tanh, R, Wb)

    def gmlp(ch, o):
        n, nseq, L = ch["n"], ch["nseq"], ch["L"]
        sample = ch["sample"]
        m0 = K.top
        Z = [K.alloc(n) for _ in range(8)]
        tmp = [K.alloc(n) for _ in range(8)]
        for q in range(8):
            gelu(Z[q].ap(0, n), pr[q].ap(o, o + n), n, tmp[q], [pr[q]], [Z[q]])
        Zu, Zv = Z[0:4], Z[4:8]
        ps = K.ps()
        for k in range(4):
            K.mm(ps.ap(0, n), ones.ap(0, 128), Zv[k].ap(0, n), k == 0, k == 3, [ones, Zv[k]], [ps])
        ps2 = K.ps()
        for k in range(4):
            K.stt(Zv[k].ap(0, n), ps.ap(0, n), -1.0 / 512, Zv[k].ap(0, n), ALU.mult, ALU.add, [ps, Zv[k]], [Zv[k]])
            t = tmp[k]
            K.tt("pool", t.ap(0, n), Zv[k].ap(0, n), Zv[k].ap(0, n), ALU.mult, [Zv[k]], [t])
            K.mm(ps2.ap(0, n), ones.ap(0, 128), t.ap(0, n), k == 0, k == 3, [ones, t], [ps2])
        rs = K.alloc(n)
        K.act(rs.ap(0, n), ps2.ap(0, n), AF.Ln, [ps2, C["pcols"]], [rs], bias=col("c_epsln"), scale=1.0 / 512)
        K.act(rs.ap(0, n), rs.ap(0, n), AF.Exp, [rs], [rs], scale=-0.5)
        for k in range(4):
            K.tt("dve", Zv[k].ap(0, n), Zv[k].ap(0, n), rs.ap(0, n), ALU.mult, [Zv[k], rs], [Zv[k]])
            K.ts("dve", Zv[k].ap(0, n), Zv[k].ap(0, n), col("ln_g", k), col("ln_b", k), ALU.mult, ALU.add, [Zv[k], C["pcols"]], [Zv[k]])
        if sample:
            fm2tm_store([(Zv[k].ap(0, n), [Zv[k]]) for k in range(4)], n, dr["o_gv_s"][:, :])
            acc = K.alloc(64)
            for h in range(4):
                vv = Zv[h].ap(0, n).rearrange("p (s l) -> p s l", l=L)
                av = acc.ap(0, n).rearrange("p (s l) -> p s l", l=L)
                for t in range(L):
                    wc = lambda tp: C["wsb"].ap(h * 16 + t * 4 + tp, h * 16 + t * 4 + tp + 1)
                    K.ts("dve", av[:, :, t], vv[:, :, 0], wc(0), C["bs_bc"].ap(h * 128 + t, h * 128 + t + 1), ALU.mult, ALU.add, [Zv[h], C["wsb"], C["bs_bc"]], [acc])
                    for tp in range(1, t + 1):
                        K.stt(av[:, :, t], vv[:, :, tp], wc(tp), av[:, :, t], ALU.mult, ALU.add, [Zv[h], acc, C["wsb"]], [acc])
                K.tt("dve", ym[h].apb(o, o + n), Zu[h].ap(0, n), acc.ap(0, n), ALU.mult, [Zu[h], acc], [ym[h]])
        else:
            for c in range(n // 128):
                c0_ = c * 128
                Vt = K.alloc(512)
                psT = K.ps()
                for k in range(4):
                    K.tr(psT.ap(k * 128, k * 128 + 128), Zv[k].ap(c0_, c0_ + 128), ident.ap(0, 128), [Zv[k], ident], [psT])
                K.cp("act", Vt.ap(0, 512), psT.ap(0, 512), [psT], [Vt])
                for h in range(4):
                    ps = K.ps()
                    K.mm(ps.ap(0, 128), Vt.ap(h * 128, h * 128 + 128), C["wmT"].ap(h * 128, h * 128 + 128), True, True, [Vt, C["wmT"]], [ps])
                    t = tmp[4 + h].sub(c0_, 128)
                    K.tt("dve", t.ap(0, 128), ps.ap(0, 128), C["bs_bc"].ap(h * 128, h * 128 + 128), ALU.add, [ps, C["bs_bc"]], [t])
                    K.tt("pool", ym[h].apb(o + c0_, o + c0_ + 128), Zu[h].ap(c0_, c0_ + 128), t.ap(0, 128), ALU.mult, [Zu[h], t], [ym[h]])
        K.top = m0

    def rglru(ch, o):
        n, nseq, L = ch["n"], ch["nseq"], ch["L"]
        sample = ch["sample"]
        Wd = 3 + L
        m0 = K.top
        rm = C["rm_s"] if sample else C["rm_p"]
        XC = [K.alloc(n) for _ in range(4)]
        GX = [K.alloc(n) for _ in range(4)]
        GA = [K.alloc(n) for _ in range(4)]
        HS = [K.alloc(n) for _ in range(4)]
        tmp = [K.alloc(n) for _ in range(4)]
        gg = [K.alloc(64) for _ in range(4)] + [K.alloc(n) for _ in range(4)]
        for k in range(4):
            e = EC[k]
            if not ch["first"]:
                K.cp("pool", v3(e, nseq, Wd, 0, 3), v3(e, nseq, Wd, L, 3), [e], [e])
            K.cp("act", v3(e, nseq, Wd, 3, L), pr[12 + k].ap(o, o + n).rearrange("p (s l) -> p s l", l=L), [pr[12 + k]], [e])
            xv = XC[k].ap(0, n).rearrange("p (s l) -> p s l", l=L)
            K.ts("dve", xv, v3(e, nseq, Wd, 0, L), col("conv_w0", k), col("conv_b", k), ALU.mult, ALU.add, [e, C["pcols"]], [XC[k]])
            for jt in range(1, 4):
                K.stt(xv, v3(e, nseq, Wd, jt, L), col("conv_w%d" % jt, k), xv, ALU.mult, ALU.add, [e, XC[k], C["pcols"]], [XC[k]])
            ps = K.ps()
            K.mm(ps.ap(0, n), C["wx_bd"].ap(k * 128, k * 128 + 128), XC[k].ap(0, n), True, True, [C["wx_bd"], XC[k]], [ps])
            K.act(GX[k].ap(0, n), ps.ap(0, n), AF.Sigmoid, [ps, C["pcols"]], [GX[k]], bias=col("bx", k))
            ps = K.ps()
            K.mm(ps.ap(0, n), C["wa_bd"].ap(k * 128, k * 128 + 128), XC[k].ap(0, n), True, True, [C["wa_bd"], XC[k]], [ps])
            K.act(GA[k].ap(0, n), ps.ap(0, n), AF.Sigmoid, [ps, C["pcols"]], [GA[k]], bias=col("ba", k))
            K.act(GA[k].ap(0, n), GA[k].ap(0, n), AF.Exp, [GA[k], nsp8], [GA[k]], scale=nsp8.ap(k, k + 1))
            t = tmp[k]
            K.tt("pool", HS[k].ap(0, n), GX[k].ap(0, n), XC[k].ap(0, n), ALU.mult, [GX[k], XC[k]], [HS[k]])
            K.tt("pool", t.ap(0, n), GA[k].ap(0, n), GA[k].ap(0, n), ALU.mult, [GA[k]], [t])
            K.ts("dve", t.ap(0, n), t.ap(0, n), 0.9999999, None, ALU.min, None, [t], [t])
            K.act(t.ap(0, n), t.ap(0, n), AF.Ln, [t, C["pcols"]], [t], bias=col("c_one"), scale=-1.0)
            K.act(t.ap(0, n), t.ap(0, n), AF.Exp, [t], [t], scale=0.5)
            K.tt("dve", t.ap(0, n), t.ap(0, n), HS[k].ap(0, n), ALU.mult, [t, HS[k]], [t])
            tv = t.ap(0, n).rearrange("p (s l) -> p s l", l=L)
            av = GA[k].ap(0, n).rearrange("p (s l) -> p s l", l=L)
            h0 = gg[k]
            K.tt("dve", h0.ap(0, nseq), av[:, :, 0], HL[k].ap(0, nseq), ALU.mult, [GA[k], HL[k]], [h0])
            K.tt("dve", tv[:, :, 0], tv[:, :, 0], h0.ap(0, nseq), ALU.add, [t, h0], [t])
            K.memset("pool", av[:, :, 0], 0.0, [GA[k]])
            K.scan(HS[k].ap(0, n), GA[k].ap(0, n), t.ap(0, n), 0.0, [GA[k], t], [HS[k]])
            K.cp("pool", HL[k].ap(0, nseq), HS[k].ap(0, n).rearrange("p (s l) -> p s l", l=L)[:, :, L - 1], [HS[k]], [HL[k]])
            g = gg[4 + k]
            gelu(g.ap(0, n), pr[8 + k].ap(o, o + n), n, None, [pr[8 + k]], [g])
            K.tt("dve", ym[4 + k].apb(o, o + n), HS[k].ap(0, n), g.ap(0, n), ALU.mult, [HS[k], g], [ym[4 + k]])
        if ch["last"]:
            stc = [K.alloc(64) for _ in range(4)]
            for k in range(4):
                K.cp("pool", stc[k].ap(0, nseq * 3).rearrange("p (s l) -> p s l", l=3), v3(EC[k], nseq, Wd, L, 3), [EC[k]], [stc[k]])
            fm2tm_store([(stc[k].ap(0, nseq * 3), [stc[k]]) for k in range(4)], nseq * 3, (dr["o_conv_s"] if sample else dr["o_conv_p"])[:, :])
            fm2tm_store([(HL[k].ap(0, nseq), [HL[k]]) for k in range(4)], nseq, (dr["o_lru_s"] if sample else dr["o_lru_p"])[:, :])
        K.top = m0

    blocks = []
    for b in range(SEQ // BN):
        chs = []
        for c in range(BLKC):
            gi = b * BLKC + c
            chs.append(dict(o=c * 128, n=128, nseq=1, L=128, first=(gi == 0), last=(gi == SEQ // 128 - 1), sample=False))
        blocks.append(dict(c0=b * BN, n=BN, chunks=chs, sample=False))
    blocks.append(dict(c0=SEQ, n=NS, chunks=[dict(o=0, n=NS, nseq=SB, L=SL, first=True, last=True, sample=True)], sample=True))

    def v3e(buf, nseq, width, c0, ln):
        return K.arena[:, buf.off:buf.off + nseq * width].rearrange("p (s w) -> p s w", w=width)[:, :, c0:c0 + ln]

    scr = [RegObj() for _ in range(8)]
    for blk in blocks:
        c0, n = blk["c0"], blk["n"]
        for r0 in range(0, n, 128):
            m = min(128, n - r0)
            tm2fm_load(dr["xin"][c0 + r0:c0 + r0 + m, :], m, 8,
                       [(xb[k].ap(r0, r0 + m), [xb[k]], None) for k in range(8)])
        if blk["sample"]:
            for j in range(2):
                K.memset("pool", EP[j].ap(), 0.0, [EP[j]])
            for r in range(2):
                tm2fm_load(dr["st_pool"][r * 120:(r + 1) * 120, :], 120, 2,
                           [(K.arena[:, EP[j].off:EP[j].off + 304].rearrange("p (s w) -> p s w", w=19)[:, r * 8:(r + 1) * 8, 0:15], [EP[j]], 15) for j in range(2)])
            for g0 in range(0, 20, 8):
                g = list(range(g0, min(20, g0 + 8)))
                tm2fm_load(dr["st_shift"][:, g0 * 128:(g0 + len(g)) * 128], SB, len(g),
                           [(K.arena[:, ES.off + q * 129:ES.off + q * 129 + SB * 5].rearrange("p (s w) -> p s w", w=5)[:, :, 0], [ES], None) for q in g])
        rmsnorm(xb, "ev_norm_g", hb, n)

        def sink0(m, ps, n=n):
            K.cp(K.ev_eng(), pr[m].ap(0, n), ps.ap(0, n), [ps], [pr[m]])

        linear("ev_w_in", None, [(hb[k].apb(0, n), [hb[k]]) for k in range(8)], 2816, n, sink0)
        for ch in blk["chunks"]:
            pool_mixer(ch, ch["o"])
            rwkv(ch, ch["o"])

        def sinkx0(m, ps, n=n):
            K.tt("dve", xb[m].ap(0, n), xb[m].ap(0, n), ps.ap(0, n), ALU.add, [xb[m], ps], [xb[m]])

        linear("ev_w_out", None, [(ym[k].apb(0, n), [ym[k]]) for k in range(8)], D, n, sinkx0)
        for k in range(8):
            K.store(xb[k], dr["xscr"][k, :, c0:c0 + n], xb[k].ap(0, n), W=[scr[k]])
    K.top = base_top
    xall = [K.alloc(NTOK) for _ in range(8)]
    for k in range(8):
        K.load(xall[k], xall[k].ap(0, NTOK), dr["xscr"][k], R=[scr[k]])
    ffn_all(0, xall)
    for blk in blocks:
        c0, n = blk["c0"], blk["n"]
        xv = [xall[k].sub(c0, n) for k in range(8)]
        if blk["sample"]:
            tm2fm_load(dr["st_conv"][:, :], SB * 3, 4,
                       [(K.arena[:, EC[k].off:EC[k].off + SB * 7].rearrange("p (s w) -> p s w", w=7)[:, :, 0:3], [EC[k]], 3) for k in range(4)])
            tm2fm_load(dr["st_lru"][:, :], SB, 4, [(HL[k].ap(0, SB), [HL[k]], None) for k in range(4)])
        rmsnorm(xv, "od_norm_g", hb, n)

        def sink1(m, ps, n=n):
            K.cp(K.ev_eng(), pr[m].ap(0, n), ps.ap(0, n), [ps], [pr[m]])

        linear("od_w_in", None, [(hb[k].apb(0, n), [hb[k]]) for k in range(8)], 2048, n, sink1)
        if blk["sample"]:
            chs1 = blk["chunks"]
        else:
            chs1 = [dict(o=0, n=n, nseq=1, L=n, first=blk["chunks"][0]["first"], last=blk["chunks"][-1]["last"], sample=False)]
        for ch in chs1:
            gmlp(ch, ch["o"])
            rglru(ch, ch["o"])

        def sinkx1(m, ps, n=n, xv=xv):
            K.tt("dve", xv[m].ap(0, n), xv[m].ap(0, n), ps.ap(0, n), ALU.add, [xv[m], ps], [xv[m]])

        linear("od_w_out", None, [(ym[k].apb(0, n), [ym[k]]) for k in range(8)], D, n, sinkx1)
    ffn_all(1, xall)
    mfin = K.top
    yn = [K.alloc(512) for _ in range(8)]
    for (t0, tn) in TBS:
        rmsnorm([xall[k].sub(t0, tn) for k in range(8)], "fin_g", yn, tn, bf=False)
        for r0 in range(0, tn, 128):
            m = min(128, tn - r0)
            fm2tm_store([(yn[k].ap(r0, r0 + m), [yn[k]]) for k in range(8)], m, dr["y"][t0 + r0:t0 + r0 + m, :])
    K.top = mfin


def _cols(v):
    v = np.asarray(v, np.float32).reshape(-1, 128)
    return np.ascontiguousarray(v.T)


def _bd(blocks):
    nt = len(blocks) // 2
    out = np.zeros((128, nt * 128), np.float32)
    for i, b in enumerate(blocks):
        t, h = i // 2, i % 2
        out[h * 64:(h + 1) * 64, t * 128 + h * 64:t * 128 + (h + 1) * 64] = b
    return out


def _host_consts(inp):
    c = {}
    pc = np.zeros((128, CONST_COLS["pcols"]), np.float32)

    def put(nm, arr):
        o, n = PC[nm]
        pc[:, o:o + n] = _cols(arr)

    put("ev_norm_g", inp["ev_norm_g"][0]); put("pool_scale", inp["pool_scale"][0]); put("mu", inp["rwkv_mu"][0])
    put("w0", inp["rwkv_w0"][0]); put("a0", inp["rwkv_a0"][0]); put("k_k", inp["rwkv_k_k"][0]); put("k_a", inp["rwkv_k_a"][0])
    put("r_k", inp["rwkv_r_k"][0].reshape(-1)); put("gn_g", inp["rwkv_gn_g"][0]); put("gn_b", inp["rwkv_gn_b"][0])
    put("od_norm_g", inp["od_norm_g"][0]); put("ln_g", inp["gmlp_ln_g"][0]); put("ln_b", inp["gmlp_ln_b"][0])
    for j in range(4):
        put("conv_w%d" % j, inp["lru_conv_w"][0, j])
    put("conv_b", inp["lru_conv_b"][0]); put("bx", inp["lru_bx"][0]); put("ba", inp["lru_ba"][0]); put("lam", inp["lru_lam"][0])
    put("ff_g0", inp["ff_norm_g"][0]); put("ff_g1", inp["ff_norm_g"][1]); put("fin_g", inp["final_norm_g"])
    pc[:, PC["c_eps6"][0]] = 1e-6
    pc[:, PC["c_epsgn"][0]] = 64e-5
    pc[:, PC["c_epsln"][0]] = 1e-5
    pc[:, PC["c_one"][0]] = 1.0
    pc[:, PC["c_tiny"][0]] = 1e-24
    c["pcols"] = pc
    c["ident"] = np.eye(128, dtype=np.float32)
    bd = np.zeros((128, 128), np.float32)
    bd[:64, :64] = 1.0
    bd[64:, 64:] = 1.0
    c["bd1"] = bd
    c["ones"] = np.ones((128, 128), np.float32)
    i = np.arange(128)
    SLm = (i[:, None] > i[None, :]).astype(np.float32)
    SU = (i[:, None] < i[None, :]).astype(np.float32)
    IU = (i[:, None] <= i[None, :]).astype(np.float32)
    c["mN_p"] = SLm
    c["m2_p"] = np.concatenate([SU, IU], 1)
    i = np.arange(64)
    same = (i[:, None] // 4 == i[None, :] // 4)
    mNs = np.zeros((128, 64), np.float32)
    mNs[:64] = (same & (i[:, None] > i[None, :])).astype(np.float32)
    c["mN_s"] = mNs
    m2s = np.zeros((128, 128), np.float32)
    m2s[:64, :64] = (same & (i[:, None] < i[None, :])).astype(np.float32)
    m2s[:64, 64:] = (same & (i[:, None] <= i[None, :])).astype(np.float32)
    c["m2_s"] = m2s
    rmp = np.ones((128, 128), np.float32)
    rmp[:, 0] = 0.0
    c["rm_p"] = rmp
    rms = np.ones((128, 64), np.float32)
    rms[:, 0::4] = 0.0
    c["rm_s"] = rms
    rmk = np.zeros((128, 16), np.float32)
    rmk[:64] = (i[:, None] // 4 == np.arange(16)[None, :]).astype(np.float32)
    c["rowmask"] = rmk
    wins = np.array([2, 4, 8, 16], np.float32)
    inv0 = np.zeros((128, 2, 128), np.float32)
    inv1 = np.zeros((128, 2, 128), np.float32)
    pos = np.arange(128, dtype=np.float32)
    for t in range(2):
        for hlf in range(2):
            w = wins[2 * t + hlf]
            inv0[hlf * 64:(hlf + 1) * 64, t, :] = 1.0 / np.minimum(w, pos + 1)[None, :]
            inv1[hlf * 64:(hlf + 1) * 64, t, :] = 1.0 / w
    c["invc0"] = inv0.reshape(128, 256)
    c["invc1"] = inv1.reshape(128, 256)
    c["wa_w2"] = np.concatenate([inp["rwkv_w_w2"][0], inp["rwkv_a_w2"][0]], 0).astype(np.float32)
    c["g_w2"] = np.ascontiguousarray(inp["rwkv_g_w2"][0])
    c["poolw_bd"] = _bd([inp["pool_w"][0, g] for g in range(4)])
    c["wx_bd"] = _bd([inp["lru_wx"][0, g] for g in range(8)])
    c["wa_bd"] = _bd([inp["lru_wa"][0, g] for g in range(8)])
    ws = inp["gmlp_ws"][0]
    c["wmT"] = np.ascontiguousarray(np.transpose(ws, (2, 0, 1)).reshape(128, 512))
    c["bs_bc"] = np.ascontiguousarray(np.broadcast_to(inp["gmlp_bs"][0].reshape(1, 512), (128, 512)))
    c["wsb"] = np.ascontiguousarray(np.broadcast_to(ws[:, :4, :4].reshape(1, 64), (128, 64)))

    return {k: np.ascontiguousarray(v, dtype=np.float32) for k, v in c.items()}


_NC_CACHE = {}


def kernel(**inp):
    inp = {k: np.asarray(v) for k, v in inp.items()}
    if "nc" not in _NC_CACHE:
        _NC_CACHE["nc"] = build_program()
    nc = _NC_CACHE["nc"]
    consts = _host_consts(inp)
    shared = dict(consts)
    shared["ev_w_in"] = np.ascontiguousarray(inp["ev_w_in"][0])
    shared["ev_w_out"] = np.ascontiguousarray(inp["ev_w_out"][0])
    shared["od_w_in"] = np.ascontiguousarray(inp["od_w_in"][0])
    shared["od_w_out"] = np.ascontiguousarray(inp["od_w_out"][0])
    shared["ff_w1"] = np.ascontiguousarray(inp["ff_w1"])
    shared["ff_w2"] = np.ascontiguousarray(inp["ff_w2"])
    in_maps = []
    for c in range(NCORE):
        s0, s1 = c * SB, (c + 1) * SB
        m = dict(shared)
        m["xin"] = np.ascontiguousarray(np.concatenate([inp["x_prompt"][c], inp["x_sample"][s0:s1].reshape(NS, D)], 0))
        m["st_pool"] = np.ascontiguousarray(inp["state_pool"][0, s0:s1].reshape(SB * 15, 256))
        m["st_shift"] = np.ascontiguousarray(inp["state_shift"][0, s0:s1])
        m["st_wkv"] = np.ascontiguousarray(inp["state_wkv"][0, s0:s1].reshape(SB * 768, 64))
        m["st_conv"] = np.ascontiguousarray(inp["state_conv"][0, s0:s1].reshape(SB * 3, 512))
        m["st_lru"] = np.ascontiguousarray(inp["state_lru"][0, s0:s1])
        in_maps.append(m)
    res = run_bass_kernel_spmd(nc, in_maps, core_ids=list(range(NCORE)))
    R = res.results
    cat = lambda nm: np.stack([np.asarray(R[c][nm]) for c in range(NCORE)], 0)
    y = cat("y")
    y_prompt = np.ascontiguousarray(y[:, :SEQ, :])
    y_sample = np.ascontiguousarray(y[:, SEQ:, :].reshape(NCORE * SB, SL, D))
    p_pool = cat("o_pool_p")[None]
    p_shift = cat("o_shift_p").reshape(1, NCORE, 2560)
    p_wkv = cat("o_wkv_p").reshape(1, NCORE, 12, 64, 64)
    p_conv = cat("o_conv_p")[None]
    p_lru = cat("o_lru_p").reshape(1, NCORE, 512)
    s_pool = cat("o_pool_s").reshape(1, NCORE * SB, 15, 256)
    s_shift = cat("o_shift_s").reshape(1, NCORE * SB, 2560)
    s_wkv = cat("o_wkv_s").reshape(1, NCORE * SB, 12, 64, 64)
    s_conv = cat("o_conv_s").reshape(1, NCORE * SB, 3, 512)
    s_lru = cat("o_lru_s").reshape(1, NCORE * SB, 512)
    s_gv = cat("o_gv_s").reshape(1, NCORE * SB, SL, 512)
    outs = (y_prompt, y_sample, p_pool, p_shift, p_wkv, p_conv, p_lru, s_pool, s_shift, s_wkv, s_conv, s_lru, s_gv)
    return tuple(np.ascontiguousarray(o, dtype=np.float32) for o in outs)
```
